# Optimizing a Trainium2 kernel written in Bass

```python
import jax, jax.numpy as jnp
from jax import lax
import numpy as np

D_MODEL = 4096
BATCH = 4
SEQ = 4096
DEPTH = 2

CHUNK = 64
BRANCH_WIDTH = D_MODEL // 2
RG_WIDTH = BRANCH_WIDTH
RG_BLOCKS = 16
RG_BLOCK_DIM = RG_WIDTH // RG_BLOCKS
RG_CONV = 4
RG_C = 8.0
SB_HEADS = 16
SB_HEAD_DIM = BRANCH_WIDTH // SB_HEADS
SB_WIDTH = SB_HEADS * SB_HEAD_DIM
Q_BLOCK = 128
N_BRANCH = 2
W_IN = 2 * RG_WIDTH + 3 * SB_WIDTH
D_FF = 3 * D_MODEL
FFN_CONV = 3
PLE_DIM = 256
EPS = 1e-6

kernel_name = "hybrid_rglru_stickbreaking_convffn_ple"


def rms_norm(x, g):
    xf = x.astype(jnp.float32)
    y = xf * lax.rsqrt(jnp.mean(xf * xf, axis=-1, keepdims=True) + EPS)
    return (y * g.astype(jnp.float32)).astype(x.dtype)


def causal_dwconv(x, w, b):
    K, C = w.shape
    y = lax.conv_general_dilated(
        x, w[:, None, :].astype(x.dtype), window_strides=(1,), padding=[(K - 1, 0)],
        dimension_numbers=("NWC", "WIO", "NWC"), feature_group_count=C)
    return y + b.astype(x.dtype)


def _linear_combine(left, right):
    a1, b1 = left
    a2, b2 = right
    return a1 * a2, a2 * b1 + b2


def rg_lru(x, w_a, b_a, w_x, b_x, lam):
    B, S, C = x.shape
    xb = x.reshape(B, S, RG_BLOCKS, RG_BLOCK_DIM)
    r = jax.nn.sigmoid(jnp.einsum("bshi,hij->bshj", xb, w_a).reshape(B, S, C) + b_a)
    i_gate = jax.nn.sigmoid(jnp.einsum("bshi,hij->bshj", xb, w_x).reshape(B, S, C) + b_x)
    log_a = -RG_C * r.astype(jnp.float32) * jax.nn.softplus(-lam.astype(jnp.float32))
    a = jnp.exp(log_a)
    mult = jnp.sqrt(-jnp.expm1(2.0 * log_a))
    u = x.astype(jnp.float32) * i_gate.astype(jnp.float32) * mult
    _, h = lax.associative_scan(_linear_combine, (a, u), axis=1)
    return h.astype(x.dtype)


def stick_breaking_attention(q, k, v):
    B, S, H, Dh = q.shape
    scale = Dh ** -0.5
    outs = []
    for blk in range(S // Q_BLOCK):
        q0 = blk * Q_BLOCK
        kv_len = q0 + Q_BLOCK
        qb = q[:, q0:kv_len].astype(jnp.float32)
        kb = k[:, :kv_len].astype(jnp.float32)
        vb = v[:, :kv_len].astype(jnp.float32)
        z = jnp.einsum("bqhd,bkhd->bhqk", qb, kb) * scale
        t_idx = q0 + jnp.arange(Q_BLOCK)[:, None]
        s_idx = jnp.arange(kv_len)[None, :]
        mask = s_idx < t_idx
        log_beta = jax.nn.log_sigmoid(z)
        log_rest = jnp.where(mask, jax.nn.log_sigmoid(-z), 0.0)
        suffix = lax.cumsum(log_rest, axis=3, reverse=True) - log_rest
        A = jnp.where(mask, jnp.exp(log_beta + suffix), 0.0)
        outs.append(jnp.einsum("bhqk,bkhd->bqhd", A, vb))
    return jnp.concatenate(outs, axis=1).astype(v.dtype)


def setup_inputs(seed: int = 0) -> dict:
    key = jax.random.key(seed)
    ks = jax.random.split(key, 26)
    f32 = jnp.float32

    def nrm(k, shape, fan_in):
        return jax.random.normal(k, shape, f32) * (fan_in ** -0.5)

    def gain(k, shape):
        return 1.0 + 0.02 * jax.random.normal(k, shape, f32)

    def bias(k, shape):
        return 0.01 * jax.random.normal(k, shape, f32)

    u = jax.random.uniform(ks[8], (DEPTH, RG_WIDTH), f32, minval=0.9, maxval=0.999)
    a0 = u ** (1.0 / RG_C)
    rg_lambda = jnp.log(a0) - jnp.log1p(-a0)

    return {
        "x": jax.random.normal(ks[0], (BATCH, SEQ, D_MODEL), f32),
        "p": jax.random.normal(ks[1], (DEPTH, BATCH, SEQ, PLE_DIM), f32),
        "g_mix": gain(ks[2], (DEPTH, D_MODEL)),
        "w_in": nrm(ks[3], (DEPTH, D_MODEL, W_IN), D_MODEL),
        "w_rg_conv": nrm(ks[4], (DEPTH, RG_CONV, RG_WIDTH), RG_CONV),
        "b_rg_conv": bias(ks[5], (DEPTH, RG_WIDTH)),
        "w_rg_a": nrm(ks[6], (DEPTH, RG_BLOCKS, RG_BLOCK_DIM, RG_BLOCK_DIM), RG_BLOCK_DIM),
        "b_rg_a": bias(ks[7], (DEPTH, RG_WIDTH)),
        "w_rg_x": nrm(ks[9], (DEPTH, RG_BLOCKS, RG_BLOCK_DIM, RG_BLOCK_DIM), RG_BLOCK_DIM),
        "b_rg_x": bias(ks[10], (DEPTH, RG_WIDTH)),
        "rg_lambda": rg_lambda,
        "w_branch": nrm(ks[11], (DEPTH, N_BRANCH, BRANCH_WIDTH, D_MODEL), BRANCH_WIDTH),
        "w_branch_gate": nrm(ks[12], (DEPTH, D_MODEL, N_BRANCH * D_MODEL), D_MODEL),
        "b_branch_gate": bias(ks[13], (DEPTH, N_BRANCH * D_MODEL)),
        "w_out": nrm(ks[14], (DEPTH, D_MODEL, D_MODEL), D_MODEL),
        "g_ffn": gain(ks[15], (DEPTH, D_MODEL)),
        "w_up": nrm(ks[16], (DEPTH, D_MODEL, 2 * D_FF), D_MODEL),
        "w_ffn_conv": nrm(ks[17], (DEPTH, FFN_CONV, 2 * D_FF), FFN_CONV),
        "b_ffn_conv": bias(ks[18], (DEPTH, 2 * D_FF)),
        "w_down": nrm(ks[19], (DEPTH, D_FF, D_MODEL), D_FF),
        "g_ple": gain(ks[20], (DEPTH, D_MODEL)),
        "w_ple": nrm(ks[21], (DEPTH, PLE_DIM, D_MODEL), PLE_DIM),
        "w_ple_gate": nrm(ks[22], (DEPTH, D_MODEL, D_MODEL), D_MODEL),
        "b_ple_gate": bias(ks[23], (DEPTH, D_MODEL)),
        "g_final": gain(ks[24], (D_MODEL,)),
    }


def reference(x, p, g_mix, w_in, w_rg_conv, b_rg_conv, w_rg_a, b_rg_a, w_rg_x, b_rg_x,
              rg_lambda, w_branch, w_branch_gate, b_branch_gate, w_out, g_ffn, w_up,
              w_ffn_conv, b_ffn_conv, w_down, g_ple, w_ple, w_ple_gate, b_ple_gate, g_final):
    B, S, _ = x.shape
    splits = [RG_WIDTH, 2 * RG_WIDTH, 2 * RG_WIDTH + SB_WIDTH, 2 * RG_WIDTH + 2 * SB_WIDTH]
    for i in range(DEPTH):
        h = rms_norm(x, g_mix[i])
        proj = h @ w_in[i]
        xr, gr, q, k, v = jnp.split(proj, splits, axis=-1)
        xr = causal_dwconv(xr, w_rg_conv[i], b_rg_conv[i])
        y_rec = rg_lru(xr, w_rg_a[i], b_rg_a[i], w_rg_x[i], b_rg_x[i], rg_lambda[i]) * jax.nn.gelu(gr)
        q = q.reshape(B, S, SB_HEADS, SB_HEAD_DIM)
        k = k.reshape(B, S, SB_HEADS, SB_HEAD_DIM)
        v = v.reshape(B, S, SB_HEADS, SB_HEAD_DIM)
        y_att = stick_breaking_attention(q, k, v).reshape(B, S, SB_WIDTH)
        branches = jnp.stack([y_rec, y_att], axis=2)
        branch_d = jnp.einsum("bsnc,ncd->bsnd", branches, w_branch[i])
        gates = jax.nn.sigmoid(h @ w_branch_gate[i] + b_branch_gate[i]).reshape(B, S, N_BRANCH, D_MODEL)
        x = x + jnp.sum(gates * branch_d, axis=2) @ w_out[i]
        h = rms_norm(x, g_ffn[i])
        up = causal_dwconv(h @ w_up[i], w_ffn_conv[i], b_ffn_conv[i])
        a_half, v_half = jnp.split(up, 2, axis=-1)
        x = x + (jax.nn.gelu(a_half) * v_half) @ w_down[i]
        e = p[i].astype(x.dtype) @ w_ple[i]
        g = jax.nn.sigmoid(rms_norm(x, g_ple[i]) @ w_ple_gate[i] + b_ple_gate[i])
        x = x + g * e
    return rms_norm(x, g_final)
```

```python
import numpy as np
from contextlib import ExitStack
import concourse.bass as bass
import concourse.mybir as mybir
from concourse.bass_utils import run_bass_kernel_spmd

F32 = mybir.dt.float32
BF16 = mybir.dt.bfloat16
I32 = mybir.dt.int32
AF = mybir.ActivationFunctionType
ALU = mybir.AluOpType

ENGS = ['pe', 'act', 'dve', 'pool', 'sp']
NS = {'sp': 16, 'act': 8, 'pool': 8, 'pe': 0, 'dve': 0}
SAME_ENGINE_SYNC = True
EPS = 1e-6


class Buf:
    __slots__ = ('name', 'w', 'r')

    def __init__(self, name=''):
        self.name = name
        self.w = None
        self.r = []


class Sched:
    def __init__(self, nc):
        self.nc = nc
        self.ins = {e: [] for e in ENGS}
        self.ndma = {e: 0 for e in ENGS}
        self.last_real = {e: None for e in ENGS}

    def add(self, eng, name, kw, reads=(), writes=(), dma=False):
        fn = (name, kw)
        deps = set()
        for b in reads:
            if b.w is not None:
                deps.add(b.w)
        for b in writes:
            if b.w is not None:
                deps.add(b.w)
            deps.update(b.r)
        idx = len(self.ins[eng])
        j = None
        if dma:
            j = self.ndma[eng]
            self.ndma[eng] += 1
            ev = ('d', eng, j)
            if j >= NS[eng]:
                deps.add(('d', eng, j - NS[eng]))
        else:
            ev = ('c', eng, idx)
            self.last_real[eng] = idx
            if eng == 'pe' or not SAME_ENGINE_SYNC:
                deps = {d for d in deps if not (d[0] == 'c' and d[1] == eng)}
        self.ins[eng].append([fn, deps, ev, False, j])
        for b in reads:
            if ev[0] == 'c':
                b.r = [e for e in b.r if not (e[0] == 'c' and e[1] == ev[1])]
            b.r.append(ev)
        for b in writes:
            b.w = ev
            b.r = []
        return ev

    def _all_pending(self):
        deps = set()
        for q in ENGS:
            n = self.ndma[q]
            for j in range(max(0, n - NS[q]), n):
                deps.add(('d', q, j))
        for e in ENGS:
            if self.last_real[e] is not None:
                deps.add(('c', e, self.last_real[e]))
        return deps

    def barrier(self):
        deps = self._all_pending()
        for e in ENGS:
            self.ins[e].append([None, set(deps), None, False, None])

    def finish(self):
        self.barrier()

    def emit(self, stack):
        nc = self.nc
        csem = {e: stack.enter_context(nc.semaphore('c_' + e)) for e in ENGS}
        dsem = {e: [stack.enter_context(nc.semaphore('d_%s_%d' % (e, i))) for i in range(NS[e])]
                for e in ENGS}
        for e in ENGS:
            for rec in self.ins[e]:
                for d in rec[1]:
                    if d[0] == 'c':
                        self.ins[d[1]][d[2]][3] = True
        val = {}
        for e in ENGS:
            c = 0
            v = []
            for rec in self.ins[e]:
                if rec[3]:
                    c += 1
                v.append(c)
            val[e] = v

        def run(eng, engine):
            waited = {}
            for fn, deps, ev, sig, j in self.ins[eng]:
                need = {}
                for d in deps:
                    if d[0] == 'c':
                        key = ('c', d[1])
                        sem = csem[d[1]]
                        v = val[d[1]][d[2]]
                    else:
                        q, jj = d[1], d[2]
                        key = ('d', q, jj % NS[q])
                        sem = dsem[q][jj % NS[q]]
                        v = 16 * (jj // NS[q] + 1)
                    if need.get(key, (None, 0))[1] < v:
                        need[key] = (sem, v)
                for key, (sem, v) in need.items():
                    if waited.get(key, 0) < v:
                        engine.wait_ge(sem, v)
                        waited[key] = v
                if fn is None:
                    continue
                kw = {k: (v(engine) if callable(v) else v) for k, v in fn[1].items()}
                inst = getattr(engine, fn[0])(**kw)
                if ev[0] == 'd':
                    inst.then_inc(dsem[eng][j % NS[eng]], 16)
                elif sig:
                    inst.then_inc(csem[eng], 1)

        with nc.Block() as block:
            @block.sync
            def _(sync):
                run('sp', sync)

            @block.scalar
            def _(scalar):
                run('act', scalar)

            @block.vector
            def _(vector):
                run('dve', vector)

            @block.gpsimd
            def _(gpsimd):
                run('pool', gpsimd)

            @block.tensor
            def _(tensor):
                run('pe', tensor)


class Cfg:
    def __init__(self, D=4096, S=4096, depth=2, RGW=2048, SBW=2048, DFF=12288, PLE=256):
        self.D, self.S, self.depth = D, S, depth
        self.RGW, self.SBW, self.DFF, self.PLE = RGW, SBW, DFF, PLE
        self.KC = D // 128
        self.NRB = RGW // 128
        self.NH = SBW // 128
        self.WIN = 2 * RGW + 3 * SBW
        self.NFF = 2 * DFF // 128
        self.TB = 512
        self.NTB = S // 512
        self.wshapes = {
            'w_in': (D, self.WIN), 'w_br': (RGW + SBW, D), 'w_bg': (D, 2 * D), 'w_out': (D, D),
            'w_up': (D, 2 * DFF), 'w_down': (DFF, D), 'w_pg': (D, D), 'w_ple': (PLE, D),
            'w_rga': (128, RGW), 'w_rgx': (128, RGW),
        }
        off = {}
        c = 0
        for nm, n in [('g_mix', self.KC), ('g_ffn', self.KC), ('g_ple', self.KC), ('g_fin', self.KC),
                      ('b_bg', 2 * self.KC), ('b_pg', self.KC), ('w_rgc', 4 * self.NRB), ('b_rgc', self.NRB),
                      ('b_rga', self.NRB), ('b_rgx', self.NRB), ('lam', self.NRB),
                      ('w_fc', 3 * self.NFF), ('b_fc', self.NFF)]:
            off[nm] = c
            c += n
        self.voff = off
        self.NV = c

    def slab_geom(self, nm):
        K, N = self.wshapes[nm]
        kc = K // 128
        KS = (kc + 31) // 32
        KCs = kc // KS
        if nm == 'w_down' and kc > 32 and kc % 16 == 0:
            KCs = 16
            KS = kc // 16
        return N // 128, KS, KCs


def slabify(W, KS, KCs):
    K, N = W.shape
    NF = N // 128
    a = W.reshape(KS, KCs, 128, NF, 128).transpose(3, 0, 2, 1, 4)
    return np.ascontiguousarray(a).reshape(NF * KS * 128, KCs * 128)


def colvec(v):
    return np.ascontiguousarray(v.reshape(-1, 128).T)


def build_program(cfg, debug_taps=(), nlayers=None, do_norm=True):
    if nlayers is None:
        nlayers = cfg.depth
    D, S, KC, TB, NTB = cfg.D, cfg.S, cfg.KC, cfg.TB, cfg.NTB
    NRB, NH, NFF, RGW, SBW, DFF = cfg.NRB, cfg.NH, cfg.NFF, cfg.RGW, cfg.SBW, cfg.DFF
    nc = bass.Bass("TRN2", target_bir_lowering=False)
    x_in = nc.dram_tensor("x", [S, D], F32, kind="ExternalInput").ap()
    p_in = nc.dram_tensor("p", [cfg.depth, S, cfg.PLE], F32, kind="ExternalInput").ap()
    y_out = nc.dram_tensor("y", [S, D], F32, kind="ExternalOutput").ap()
    vec_in = nc.dram_tensor("vec", [128, cfg.depth * cfg.NV], F32, kind="ExternalInput").ap()
    wf, wb = {}, {}
    for l in range(nlayers):
        for nm in cfg.wshapes:
            NF, KS, KCs = cfg.slab_geom(nm)
            shp = [NF * KS * 128, KCs * 128]
            wf[l, nm] = nc.dram_tensor("%s_%d" % (nm, l), shp, F32, kind="ExternalInput").ap()
            wb[l, nm] = nc.dram_tensor("b%s_%d" % (nm, l), shp, BF16, kind="Internal").ap()

    def scratch(nm, shp, dt):
        kind = "ExternalOutput" if nm in debug_taps else "Internal"
        return nc.dram_tensor(nm, shp, dt, kind=kind).ap()
    XT = [scratch("XT0", [D, S], F32), scratch("XT1", [D, S], F32), scratch("XT2", [D, S], F32)]
    PXG = scratch("PXG", [2 * RGW, S], F32)
    PQKV = scratch("PQKV", [3 * SBW, S], BF16)
    YBR = scratch("YBR", [RGW + SBW, S], BF16)

    with ExitStack() as st:
        Sx = Sched(nc)
        AW = 44 * 1024
        arena = st.enter_context(nc.sbuf_tensor("arena", [128, AW], F32))
        banks = [st.enter_context(nc.psum_tensor("bank%d" % i, [128, 512], F32)) for i in range(8)]
        bbuf = [Buf("bank%d" % i) for i in range(8)]

        class Arena:
            def __init__(self, base):
                self.base = base
                self.cur = base

            def f32(self, n):
                a = arena[:, self.cur:self.cur + n]
                self.cur += n
                assert self.cur <= AW, self.cur
                return a

            def bf16(self, n):
                w = (n + 1) // 2
                a = arena[:, self.cur:self.cur + w].bitcast(BF16)
                self.cur += w
                assert self.cur <= AW, self.cur
                return a

            def i32(self, n):
                a = arena[:, self.cur:self.cur + n].bitcast(I32)
                self.cur += n
                return a

        P = Arena(0)
        vec = P.f32(cfg.depth * cfg.NV)
        ident_f = P.f32(128)
        ident_b = P.bf16(128)
        ones_b = P.bf16(128)
        L1 = P.bf16(128)
        L2 = P.bf16(128)
        maskb = P.bf16(4 * 512)
        negmb = P.bf16(4 * 512)
        epsc = P.f32(1)
        onec = P.f32(1)
        rgsc = P.f32(cfg.depth * NRB)
        tmpi = P.i32(512)
        tmpf = P.f32(512)
        cb = Buf('const')
        Sx.add('sp', 'dma_start', dict(out=vec, in_=vec_in), writes=[cb], dma=True)
        Sx.add('dve', 'memset', dict(ap=epsc, constant=EPS), writes=[cb])
        Sx.add('dve', 'memset', dict(ap=onec, constant=1.0), writes=[cb])
        Sx.add('dve', 'memset', dict(ap=ones_b, constant=1.0), writes=[cb])
        tb_ = Buf('tmp')
        Sx.add('pool', 'iota', dict(out=tmpi[:, 0:128], pattern=[[1, 128]], base=0, channel_multiplier=-1), writes=[tb_])
        Sx.add('dve', 'tensor_single_scalar', dict(out=ident_f, in_=tmpi[:, 0:128], scalar=0, op=ALU.is_equal), reads=[tb_], writes=[cb])
        Sx.add('dve', 'tensor_single_scalar', dict(out=ident_b, in_=tmpi[:, 0:128], scalar=0, op=ALU.is_equal), reads=[tb_], writes=[cb])
        Sx.add('dve', 'tensor_single_scalar', dict(out=L1, in_=tmpi[:, 0:128], scalar=0, op=ALU.is_lt), reads=[tb_], writes=[cb])
        Sx.add('dve', 'tensor_single_scalar', dict(out=L2, in_=tmpi[:, 0:128], scalar=0, op=ALU.is_ge), reads=[tb_], writes=[cb])
        for j in range(4):
            Sx.add('pool', 'iota', dict(out=tmpi, pattern=[[1, 512]], base=-128 * j, channel_multiplier=-1), reads=[cb], writes=[tb_])
            Sx.add('dve', 'tensor_single_scalar', dict(out=maskb[:, j * 512:(j + 1) * 512], in_=tmpi, scalar=0, op=ALU.is_gt), reads=[tb_], writes=[cb])
            Sx.add('dve', 'tensor_scalar', dict(out=negmb[:, j * 512:(j + 1) * 512], in0=maskb[:, j * 512:(j + 1) * 512], scalar1=-1.0, scalar2=3000.0, op0=ALU.add, op1=ALU.mult), reads=[cb], writes=[cb])
        for l in range(cfg.depth):
            lo = l * cfg.NV + cfg.voff['lam']
            Sx.add('act', 'activation', dict(out=rgsc[:, l * NRB:(l + 1) * NRB], in_=vec[:, lo:lo + NRB], func=AF.Exp, scale=-1.0), reads=[cb], writes=[cb])
            Sx.add('act', 'activation', dict(out=rgsc[:, l * NRB:(l + 1) * NRB], in_=rgsc[:, l * NRB:(l + 1) * NRB], func=AF.Ln, bias=onec, scale=1.0), reads=[cb], writes=[cb])
            Sx.add('dve', 'tensor_scalar', dict(out=rgsc[:, l * NRB:(l + 1) * NRB], in0=rgsc[:, l * NRB:(l + 1) * NRB], scalar1=-8.0, scalar2=None, op0=ALU.mult), reads=[cb], writes=[cb])
        PBASE = P.cur

        for l in range(nlayers):
            for nm in cfg.wshapes:
                src, dst = wf[l, nm], wb[l, nm]
                R, L = src.shape
                if L > 2048:
                    src = src.rearrange("r (a b) -> (r a) b", b=2048)
                    dst = dst.rearrange("r (a b) -> (r a) b", b=2048)
                    R = R * (L // 2048)
                for r0 in range(0, R, 8192):
                    r1 = min(R, r0 + 8192)
                    Sx.add('pool', 'dma_start', dict(out=dst[r0:r1, :], in_=src[r0:r1, :]), dma=True)
        Sx.barrier()

        bank_rr = [0]

        def next_bank():
            i = bank_rr[0]
            bank_rr[0] = (i + 1) % 8
            return i

        class Rot:
            def __init__(self, tiles):
                self.t = tiles
                self.b = [Buf() for _ in tiles]
                self.i = 0

            def next(self):
                k = self.i
                self.i = (k + 1) % len(self.t)
                return self.t[k], self.b[k]

        def vcol(l, nm, c):
            o = l * cfg.NV + cfg.voff[nm] + c
            return vec[:, o:o + 1]

        def load_slab(l, nm, fc, ks, wrot):
            NF, KS, KCs = cfg.slab_geom(nm)
            t, b = wrot.next()
            r0 = (fc * KS + ks) * 128
            src = wb[l, nm][r0:r0 + 128, :]
            Sx.add('sp', 'dma_start', dict(out=t[:, 0:KCs * 128], in_=src), writes=[b], dma=True)
            return t, b, KCs

        def norm_pass(A, xsrc, t0, l, gname, hT, hb, xrot, sqrot):
            bi = next_bank()
            ss, ssb = banks[bi], bbuf[bi]
            for c in range(KC):
                xt, xb = xrot.next()
                Sx.add('sp', 'dma_start', dict(out=xt, in_=xsrc[c * 128:(c + 1) * 128, t0:t0 + TB]), writes=[xb], dma=True)
                sq, sqb = sqrot.next()
                Sx.add('act', 'activation', dict(out=sq, in_=xt, func=AF.Square), reads=[xb], writes=[sqb])
                Sx.add('pe', 'matmul', dict(out=ss[:], lhsT=ones_b, rhs=sq, start=(c == 0), stop=(c == KC - 1)), reads=[sqb, cb], writes=[ssb])
            rstd, rb = A['rstd']
            Sx.add('act', 'activation', dict(out=rstd, in_=ss[:], func=AF.Sqrt, bias=epsc, scale=1.0 / D), reads=[ssb, cb], writes=[rb])
            Sx.add('dve', 'reciprocal', dict(out=rstd, in_=rstd), reads=[rb], writes=[rb])
            for c in range(KC):
                xt, xb = xrot.next()
                Sx.add('sp', 'dma_start', dict(out=xt, in_=xsrc[c * 128:(c + 1) * 128, t0:t0 + TB]), writes=[xb], dma=True)
                Sx.add('dve', 'scalar_tensor_tensor', dict(out=hT[:, c * TB:(c + 1) * TB], in0=xt, scalar=vcol(l, gname, c), in1=rstd, op0=ALU.mult, op1=ALU.mult), reads=[xb, rb, cb], writes=[hb])

        def mm_group(bi, specs, rhs_fn, rhs_bufs):
            n = sum(cfg.slab_geom(sp_[1])[2] for sp_ in specs)
            i = 0
            for sp_ in specs:
                t, b, KCs = load_slab(*sp_)
                for kc in range(KCs):
                    Sx.add('pe', 'matmul', dict(out=banks[bi][:], lhsT=t[:, kc * 128:(kc + 1) * 128], rhs=rhs_fn(i), start=(i == 0), stop=(i == n - 1)),
                           reads=[b] + rhs_bufs, writes=[bbuf[bi]])
                    i += 1

        def phase_input():
            A = Arena(PBASE)
            xin = Rot([A.f32(D) for _ in range(2)])
            stg = Rot([A.f32(512) for _ in range(4)])
            for tt in range(S // 128):
                xt, xb = xin.next()
                Sx.add('sp', 'dma_start', dict(out=xt, in_=x_in[tt * 128:(tt + 1) * 128, :]), writes=[xb], dma=True)
                for c4 in range(KC // 4):
                    bi = next_bank()
                    for k in range(4):
                        c = c4 * 4 + k
                        Sx.add('pe', 'transpose', dict(out=banks[bi][:, k * 128:(k + 1) * 128], in_=xt[:, c * 128:(c + 1) * 128], identity=ident_f), reads=[xb, cb], writes=[bbuf[bi]])
                    sg, sb = stg.next()
                    Sx.add('act', 'activation', dict(out=sg, in_=banks[bi][:], func=AF.Copy), reads=[bbuf[bi]], writes=[sb])
                    dst = XT[0][c4 * 512:(c4 + 1) * 512, tt * 128:(tt + 1) * 128].rearrange("(k p) t -> p k t", p=128)
                    Sx.add('pool', 'dma_start', dict(out=dst, in_=sg.rearrange("p (k t) -> p k t", t=128)), reads=[sb], dma=True)
            Sx.barrier()

        def phase_A(l, xsrc):
            A = Arena(PBASE)
            hT = A.bf16(KC * TB)
            hb = Buf('hT')
            xrot = Rot([A.f32(TB) for _ in range(4)])
            sqrot = Rot([A.bf16(TB) for _ in range(2)])
            AA = {'rstd': (A.f32(TB), Buf())}
            wrot = Rot([A.bf16(32 * 128) for _ in range(4)])
            sf = Rot([A.f32(TB) for _ in range(3)])
            sbh = Rot([A.bf16(TB) for _ in range(3)])
            for tb in range(NTB):
                t0 = tb * TB
                norm_pass(AA, xsrc, t0, l, 'g_mix', hT, hb, xrot, sqrot)
                for fc in range(cfg.WIN // 128):
                    bi = next_bank()
                    mm_group(bi, [(l, 'w_in', fc, 0, wrot)], lambda kc: hT[:, kc * TB:(kc + 1) * TB], [hb])
                    if fc < 2 * NRB:
                        sg, sb = sf.next()
                        dst = PXG[fc * 128:(fc + 1) * 128, t0:t0 + TB]
                    else:
                        sg, sb = sbh.next()
                        f2 = fc - 2 * NRB
                        dst = PQKV[f2 * 128:(f2 + 1) * 128, t0:t0 + TB]
                    Sx.add('act', 'activation', dict(out=sg, in_=banks[bi][:], func=AF.Copy), reads=[bbuf[bi]], writes=[sb])
                    Sx.add('pool', 'dma_start', dict(out=dst, in_=sg), reads=[sb], dma=True)
            Sx.barrier()

        def phase_RG(l):
            A = Arena(PBASE)
            TC = min(1024, S)
            NTC = S // TC
            xr_r = Rot([A.f32(TC + 3) for _ in range(2)])
            gr_r = Rot([A.f32(TC) for _ in range(2)])
            xc, xcB = A.f32(TC), Buf()
            xcb, xcbB = A.bf16(TC), Buf()
            rg, rgB = A.f32(TC), Buf()
            ig, igB = A.f32(TC), Buf()
            av, avB = A.f32(TC), Buf()
            mu, muB = A.f32(TC), Buf()
            hh_r = Rot([A.f32(TC) for _ in range(2)])
            ge, geB = A.f32(TC), Buf()
            yo_r = Rot([A.bf16(TC) for _ in range(2)])
            wa_r = Rot([A.bf16(128) for _ in range(2)])
            wx_r = Rot([A.bf16(128) for _ in range(2)])
            for cbk in range(NRB):
                wa, waB, _ = load_slab(l, 'w_rga', cbk, 0, wa_r)
                wx, wxB, _ = load_slab(l, 'w_rgx', cbk, 0, wx_r)
                prev_h = None
                for tcn in range(NTC):
                    t0 = tcn * TC
                    xr, xrB = xr_r.next()
                    gr, grB = gr_r.next()
                    if tcn == 0:
                        Sx.add('dve', 'memset', dict(ap=xr[:, 0:3], constant=0.0), writes=[xrB])
                        Sx.add('sp', 'dma_start', dict(out=xr[:, 3:3 + TC], in_=PXG[cbk * 128:(cbk + 1) * 128, 0:TC]), writes=[xrB], dma=True)
                    else:
                        Sx.add('sp', 'dma_start', dict(out=xr[:, 0:3 + TC], in_=PXG[cbk * 128:(cbk + 1) * 128, t0 - 3:t0 + TC]), writes=[xrB], dma=True)
                    Sx.add('sp', 'dma_start', dict(out=gr, in_=PXG[RGW + cbk * 128:RGW + (cbk + 1) * 128, t0:t0 + TC]), writes=[grB], dma=True)
                    Sx.add('dve', 'tensor_scalar', dict(out=xc, in0=xr[:, 3:3 + TC], scalar1=vcol(l, 'w_rgc', cbk * 4 + 3), scalar2=vcol(l, 'b_rgc', cbk), op0=ALU.mult, op1=ALU.add), reads=[xrB, cb], writes=[xcB])
                    for k in range(3):
                        Sx.add('dve', 'scalar_tensor_tensor', dict(out=xc, in0=xr[:, k:k + TC], scalar=vcol(l, 'w_rgc', cbk * 4 + k), in1=xc, op0=ALU.mult, op1=ALU.add), reads=[xrB, cb, xcB], writes=[xcB])
                    Sx.add('pool', 'tensor_copy', dict(out=xcb, in_=xc), reads=[xcB], writes=[xcbB])
                    for sub in range(TC // 512):
                        sl = slice(sub * 512, (sub + 1) * 512)
                        b1 = next_bank()
                        Sx.add('pe', 'matmul', dict(out=banks[b1][:], lhsT=wa, rhs=xcb[:, sl], start=True, stop=True), reads=[waB, xcbB], writes=[bbuf[b1]])
                        Sx.add('act', 'activation', dict(out=rg[:, sl], in_=banks[b1][:], func=AF.Sigmoid, bias=vcol(l, 'b_rga', cbk), scale=1.0), reads=[bbuf[b1], cb], writes=[rgB])
                        b2 = next_bank()
                        Sx.add('pe', 'matmul', dict(out=banks[b2][:], lhsT=wx, rhs=xcb[:, sl], start=True, stop=True), reads=[wxB, xcbB], writes=[bbuf[b2]])
                        Sx.add('act', 'activation', dict(out=ig[:, sl], in_=banks[b2][:], func=AF.Sigmoid, bias=vcol(l, 'b_rgx', cbk), scale=1.0), reads=[bbuf[b2], cb], writes=[igB])
                    Sx.add('act', 'activation', dict(out=av, in_=rg, func=AF.Exp, scale=rgsc[:, l * NRB + cbk:l * NRB + cbk + 1]), reads=[rgB, cb], writes=[avB])
                    Sx.add('dve', 'tensor_tensor', dict(out=mu, in0=av, in1=av, op=ALU.mult), reads=[avB], writes=[muB])
                    Sx.add('act', 'activation', dict(out=mu, in_=mu, func=AF.Sqrt, bias=onec, scale=-1.0), reads=[muB, cb], writes=[muB])
                    Sx.add('dve', 'tensor_tensor', dict(out=ig, in0=ig, in1=xc, op=ALU.mult), reads=[igB, xcB], writes=[igB])
                    Sx.add('dve', 'tensor_tensor', dict(out=ig, in0=ig, in1=mu, op=ALU.mult), reads=[igB, muB], writes=[igB])
                    hh, hhB = hh_r.next()
                    if prev_h is None:
                        Sx.add('dve', 'tensor_tensor_scan', dict(out=hh, data0=av, data1=ig, initial=0.0, op0=ALU.mult, op1=ALU.add), reads=[avB, igB], writes=[hhB])
                    else:
                        ph, phB = prev_h
                        Sx.add('dve', 'tensor_tensor_scan', dict(out=hh, data0=av, data1=ig, initial=ph[:, TC - 1:TC], op0=ALU.mult, op1=ALU.add), reads=[avB, igB, phB], writes=[hhB])
                    prev_h = (hh, hhB)
                    Sx.add('act', 'activation', dict(out=ge, in_=gr, func=AF.Gelu_apprx_tanh), reads=[grB], writes=[geB])
                    yo, yoB = yo_r.next()
                    Sx.add('dve', 'tensor_tensor', dict(out=yo, in0=hh, in1=ge, op=ALU.mult), reads=[hhB, geB], writes=[yoB])
                    Sx.add('pool', 'dma_start', dict(out=YBR[cbk * 128:(cbk + 1) * 128, t0:t0 + TC], in_=yo), reads=[yoB], dma=True)
            Sx.barrier()

        def phase_ATT(l):
            A = Arena(PBASE)
            scale = 128.0 ** -0.5
            qT, qB = A.bf16(S), Buf()
            kT, kB = A.bf16(S), Buf()
            vT, vTB = A.bf16(S), Buf()
            vv, vB = A.bf16(S), Buf()
            e_r = Rot([A.f32(512) for _ in range(2)])
            sp_r = Rot([A.f32(512) for _ in range(2)])
            spb_r = Rot([A.bf16(512) for _ in range(2)])
            t_r = Rot([A.f32(512) for _ in range(2)])
            a_r = Rot([A.bf16(512) for _ in range(2)])
            yo_r = Rot([A.bf16(512) for _ in range(2)])
            NKB = S // 128
            for hd in range(NH):
                Sx.add('sp', 'dma_start', dict(out=qT, in_=PQKV[hd * 128:(hd + 1) * 128, :]), writes=[qB], dma=True)
                Sx.add('sp', 'dma_start', dict(out=kT, in_=PQKV[SBW + hd * 128:SBW + (hd + 1) * 128, :]), writes=[kB], dma=True)
                Sx.add('sp', 'dma_start', dict(out=vT, in_=PQKV[2 * SBW + hd * 128:2 * SBW + (hd + 1) * 128, :]), writes=[vTB], dma=True)
                for k8 in range(NKB // 8):
                    bi = next_bank()
                    pb = banks[bi][:].bitcast(BF16)
                    for k in range(8):
                        kb = k8 * 8 + k
                        Sx.add('pe', 'transpose', dict(out=pb[:, k * 128:(k + 1) * 128], in_=vT[:, kb * 128:(kb + 1) * 128], identity=ident_b), reads=[vTB, cb], writes=[bbuf[bi]])
                    Sx.add('act', 'activation', dict(out=vv[:, k8 * 1024:(k8 + 1) * 1024], in_=pb[:, 0:1024], func=AF.Copy), reads=[bbuf[bi]], writes=[vB])
                for qg in range(S // 512):
                    q0 = qg * 512
                    ysb = next_bank()
                    sab = next_bank()
                    kbs = list(range(4 * qg + 3, -1, -1))
                    for n, kb in enumerate(kbs):
                        first, last = (n == 0), (n == len(kbs) - 1)
                        jd = kb - 4 * qg
                        zb = next_bank()
                        while zb in (ysb, sab):
                            zb = next_bank()
                        Sx.add('pe', 'matmul', dict(out=banks[zb][:], lhsT=kT[:, kb * 128:(kb + 1) * 128], rhs=qT[:, q0:q0 + 512], start=True, stop=True), reads=[kB, qB], writes=[bbuf[zb]])
                        ee, eB = e_r.next()
                        sp, spB = sp_r.next()
                        spb, spbB = spb_r.next()
                        Sx.add('act', 'activation', dict(out=ee, in_=banks[zb][:], func=AF.Exp, scale=scale), reads=[bbuf[zb]], writes=[eB])
                        Sx.add('act', 'activation', dict(out=sp, in_=ee, func=AF.Ln, bias=onec, scale=1.0), reads=[eB, cb], writes=[spB])
                        if jd >= 0:
                            Sx.add('dve', 'tensor_tensor', dict(out=spb, in0=sp, in1=maskb[:, jd * 512:(jd + 1) * 512], op=ALU.mult), reads=[spB, cb], writes=[spbB])
                        else:
                            Sx.add('pool', 'tensor_copy', dict(out=spb, in_=sp), reads=[spB], writes=[spbB])
                        Sx.add('pe', 'matmul', dict(out=banks[sab][:], lhsT=L1, rhs=spb, start=first, stop=False), reads=[spbB, cb], writes=[bbuf[sab]])
                        tt_, tB = t_r.next()
                        Sx.add('dve', 'scalar_tensor_tensor', dict(out=tt_, in0=banks[zb][:], scalar=scale, in1=sp, op0=ALU.mult, op1=ALU.subtract), reads=[bbuf[zb], spB], writes=[tB])
                        Sx.add('dve', 'tensor_tensor', dict(out=tt_, in0=tt_, in1=banks[sab][:], op=ALU.subtract), reads=[tB, bbuf[sab]], writes=[tB])
                        Sx.add('pe', 'matmul', dict(out=banks[sab][:], lhsT=L2, rhs=spb, start=False, stop=last), reads=[spbB, cb], writes=[bbuf[sab]])
                        if jd >= 0:
                            Sx.add('dve', 'tensor_tensor', dict(out=tt_, in0=tt_, in1=negmb[:, jd * 512:(jd + 1) * 512], op=ALU.add), reads=[tB, cb], writes=[tB])
                        aa, aB = a_r.next()
                        Sx.add('act', 'activation', dict(out=aa, in_=tt_, func=AF.Exp), reads=[tB], writes=[aB])
                        Sx.add('pe', 'matmul', dict(out=banks[ysb][:], lhsT=vv[:, kb * 128:(kb + 1) * 128], rhs=aa, start=first, stop=last), reads=[vB, aB], writes=[bbuf[ysb]])
                    yo, yoB = yo_r.next()
                    Sx.add('act', 'activation', dict(out=yo, in_=banks[ysb][:], func=AF.Copy), reads=[bbuf[ysb]], writes=[yoB])
                    Sx.add('pool', 'dma_start', dict(out=YBR[RGW + hd * 128:RGW + (hd + 1) * 128, q0:q0 + 512], in_=yo), reads=[yoB], dma=True)
            Sx.barrier()

        def phase_C(l, xsrc, xdst):
            A = Arena(PBASE)
            hT, hb = A.bf16(KC * TB), Buf()
            NBC = (RGW + SBW) // 128
            yT, yb = A.bf16(NBC * TB), Buf()
            mT, mb = A.bf16(KC * TB), Buf()
            xrot = Rot([A.f32(TB) for _ in range(4)])
            sqrot = Rot([A.bf16(TB) for _ in range(2)])
            AA = {'rstd': (A.f32(TB), Buf())}
            wrot = Rot([A.bf16(32 * 128) for _ in range(4)])
            s0_r = Rot([A.f32(TB) for _ in range(2)])
            s1_r = Rot([A.f32(TB) for _ in range(2)])
            so_r = Rot([A.f32(TB) for _ in range(2)])
            NRC = RGW // 128
            for tb in range(NTB):
                t0 = tb * TB
                norm_pass(AA, xsrc, t0, l, 'g_mix', hT, hb, xrot, sqrot)
                src = YBR[:, t0:t0 + TB].rearrange("(k p) t -> p k t", p=128)
                Sx.add('sp', 'dma_start', dict(out=yT.rearrange("p (k t) -> p k t", t=TB), in_=src), writes=[yb], dma=True)
                for fc in range(KC):
                    sbr = load_slab(l, 'w_br', fc, 0, wrot)
                    t, b, KCs = sbr
                    bd0, bd1 = next_bank(), next_bank()
                    for n_, (bi, k0, k1) in enumerate([(bd0, 0, NRC), (bd1, NRC, NBC)]):
                        for kc in range(k0, k1):
                            Sx.add('pe', 'matmul', dict(out=banks[bi][:], lhsT=t[:, kc * 128:(kc + 1) * 128], rhs=yT[:, kc * TB:(kc + 1) * TB], start=(kc == k0), stop=(kc == k1 - 1)), reads=[b, yb], writes=[bbuf[bi]])
                    g0, g1 = next_bank(), next_bank()
                    mm_group(g0, [(l, 'w_bg', fc, 0, wrot)], lambda kc: hT[:, kc * TB:(kc + 1) * TB], [hb])
                    mm_group(g1, [(l, 'w_bg', KC + fc, 0, wrot)], lambda kc: hT[:, kc * TB:(kc + 1) * TB], [hb])
                    s0, s0B = s0_r.next()
                    s1, s1B = s1_r.next()
                    Sx.add('act', 'activation', dict(out=s0, in_=banks[g0][:], func=AF.Sigmoid, bias=vcol(l, 'b_bg', fc), scale=1.0), reads=[bbuf[g0], cb], writes=[s0B])
                    Sx.add('act', 'activation', dict(out=s1, in_=banks[g1][:], func=AF.Sigmoid, bias=vcol(l, 'b_bg', KC + fc), scale=1.0), reads=[bbuf[g1], cb], writes=[s1B])
                    Sx.add('dve', 'tensor_tensor', dict(out=s0, in0=s0, in1=banks[bd0][:], op=ALU.mult), reads=[s0B, bbuf[bd0]], writes=[s0B])
                    Sx.add('dve', 'tensor_tensor', dict(out=s1, in0=s1, in1=banks[bd1][:], op=ALU.mult), reads=[s1B, bbuf[bd1]], writes=[s1B])
                    Sx.add('dve', 'tensor_tensor', dict(out=mT[:, fc * TB:(fc + 1) * TB], in0=s0, in1=s1, op=ALU.add), reads=[s0B, s1B], writes=[mb])
                for fc in range(KC):
                    bi = next_bank()
                    mm_group(bi, [(l, 'w_out', fc, 0, wrot)], lambda kc: mT[:, kc * TB:(kc + 1) * TB], [mb])
                    xt, xb = xrot.next()
                    Sx.add('sp', 'dma_start', dict(out=xt, in_=xsrc[fc * 128:(fc + 1) * 128, t0:t0 + TB]), writes=[xb], dma=True)
                    so, soB = so_r.next()
                    Sx.add('dve', 'tensor_tensor', dict(out=so, in0=xt, in1=banks[bi][:], op=ALU.add), reads=[xb, bbuf[bi]], writes=[soB])
                    Sx.add('pool', 'dma_start', dict(out=xdst[fc * 128:(fc + 1) * 128, t0:t0 + TB], in_=so), reads=[soB], dma=True)
            Sx.barrier()

        def phase_D(l, xsrc, xres, xdst, j0, j1):
            A = Arena(PBASE)
            hT, hb = A.bf16(KC * TB), Buf()
            NJ = DFF // 128
            nj = j1 - j0
            aT, ab = A.bf16(nj * TB), Buf()
            xrot = Rot([A.f32(TB) for _ in range(4)])
            sqrot = Rot([A.bf16(TB) for _ in range(2)])
            AA = {'rstd': (A.f32(TB), Buf())}
            wrot = Rot([A.bf16(32 * 128) for _ in range(4)])
            HL, HLB = A.f32(NFF * 2), Buf()
            ua_r = Rot([A.f32(TB + 2) for _ in range(2)])
            uv_r = Rot([A.f32(TB + 2) for _ in range(2)])
            ca_r = Rot([A.f32(TB) for _ in range(2)])
            cv_r = Rot([A.f32(TB) for _ in range(2)])
            so_r = Rot([A.f32(TB) for _ in range(2)])
            Sx.add('dve', 'memset', dict(ap=HL, constant=0.0), writes=[HLB])
            _, KSd, KCd = cfg.slab_geom('w_down')
            assert j0 % KCd == 0 and j1 % KCd == 0
            for tb in range(NTB):
                t0 = tb * TB
                norm_pass(AA, xsrc, t0, l, 'g_ffn', hT, hb, xrot, sqrot)
                for j in range(j0, j1):
                    res = []
                    for half, rot_u, rot_c in [(0, ua_r, ca_r), (1, uv_r, cv_r)]:
                        f = half * NJ + j
                        bi = next_bank()
                        mm_group(bi, [(l, 'w_up', f, 0, wrot)], lambda kc: hT[:, kc * TB:(kc + 1) * TB], [hb])
                        u, uB = rot_u.next()
                        c_, cB = rot_c.next()
                        Sx.add('pool', 'tensor_copy', dict(out=u[:, 0:2], in_=HL[:, f * 2:f * 2 + 2]), reads=[HLB], writes=[uB])
                        Sx.add('act', 'activation', dict(out=u[:, 2:2 + TB], in_=banks[bi][:], func=AF.Copy), reads=[bbuf[bi]], writes=[uB])
                        Sx.add('pool', 'tensor_copy', dict(out=HL[:, f * 2:f * 2 + 2], in_=u[:, TB:TB + 2]), reads=[uB], writes=[HLB])
                        Sx.add('dve', 'tensor_scalar', dict(out=c_, in0=u[:, 2:2 + TB], scalar1=vcol(l, 'w_fc', f * 3 + 2), scalar2=vcol(l, 'b_fc', f), op0=ALU.mult, op1=ALU.add), reads=[uB, cb], writes=[cB])
                        for k in range(2):
                            Sx.add('dve', 'scalar_tensor_tensor', dict(out=c_, in0=u[:, k:k + TB], scalar=vcol(l, 'w_fc', f * 3 + k), in1=c_, op0=ALU.mult, op1=ALU.add), reads=[uB, cb, cB], writes=[cB])
                        res.append((c_, cB))
                    (ca, caB), (cv, cvB) = res
                    Sx.add('act', 'activation', dict(out=ca, in_=ca, func=AF.Gelu_apprx_tanh), reads=[caB], writes=[caB])
                    Sx.add('dve', 'tensor_tensor', dict(out=aT[:, (j - j0) * TB:(j - j0 + 1) * TB], in0=ca, in1=cv, op=ALU.mult), reads=[caB, cvB], writes=[ab])
                for fc in range(KC):
                    bi = next_bank()
                    specs = [(l, 'w_down', fc, ks, wrot) for ks in range(j0 // KCd, j1 // KCd)]
                    mm_group(bi, specs, lambda kc: aT[:, kc * TB:(kc + 1) * TB], [ab])
                    xt, xb = xrot.next()
                    Sx.add('sp', 'dma_start', dict(out=xt, in_=xres[fc * 128:(fc + 1) * 128, t0:t0 + TB]), writes=[xb], dma=True)
                    so, soB = so_r.next()
                    Sx.add('dve', 'tensor_tensor', dict(out=so, in0=xt, in1=banks[bi][:], op=ALU.add), reads=[xb, bbuf[bi]], writes=[soB])
                    Sx.add('pool', 'dma_start', dict(out=xdst[fc * 128:(fc + 1) * 128, t0:t0 + TB], in_=so), reads=[soB], dma=True)
            Sx.barrier()

        def phase_E(l, xsrc, xdst):
            A = Arena(PBASE)
            hT, hb = A.bf16(KC * TB), Buf()
            NPC = cfg.PLE // 128
            pT, pb_ = A.bf16(NPC * TB), Buf()
            pin_r = Rot([A.f32(cfg.PLE) for _ in range(2)])
            xrot = Rot([A.f32(TB) for _ in range(4)])
            sqrot = Rot([A.bf16(TB) for _ in range(2)])
            AA = {'rstd': (A.f32(TB), Buf())}
            wrot = Rot([A.bf16(32 * 128) for _ in range(4)])
            wprot = Rot([A.bf16(NPC * 128) for _ in range(2)])
            s_r = Rot([A.f32(TB) for _ in range(2)])
            so_r = Rot([A.f32(TB) for _ in range(2)])
            for tb in range(NTB):
                t0 = tb * TB
                norm_pass(AA, xsrc, t0, l, 'g_ple', hT, hb, xrot, sqrot)
                for tt in range(TB // 128):
                    pin, pinB = pin_r.next()
                    Sx.add('sp', 'dma_start', dict(out=pin, in_=p_in[l, t0 + tt * 128:t0 + (tt + 1) * 128, :]), writes=[pinB], dma=True)
                    bi = next_bank()
                    for k in range(NPC):
                        Sx.add('pe', 'transpose', dict(out=banks[bi][:, k * 128:(k + 1) * 128], in_=pin[:, k * 128:(k + 1) * 128], identity=ident_f), reads=[pinB, cb], writes=[bbuf[bi]])
                    for k in range(NPC):
                        Sx.add('act', 'activation', dict(out=pT[:, k * TB + tt * 128:k * TB + (tt + 1) * 128], in_=banks[bi][:, k * 128:(k + 1) * 128], func=AF.Copy), reads=[bbuf[bi]], writes=[pb_])
                for fc in range(KC):
                    gb = next_bank()
                    mm_group(gb, [(l, 'w_pg', fc, 0, wrot)], lambda kc: hT[:, kc * TB:(kc + 1) * TB], [hb])
                    eb = next_bank()
                    mm_group(eb, [(l, 'w_ple', fc, 0, wprot)], lambda kc: pT[:, kc * TB:(kc + 1) * TB], [pb_])
                    s, sB = s_r.next()
                    Sx.add('act', 'activation', dict(out=s, in_=banks[gb][:], func=AF.Sigmoid, bias=vcol(l, 'b_pg', fc), scale=1.0), reads=[bbuf[gb], cb], writes=[sB])
                    Sx.add('dve', 'tensor_tensor', dict(out=s, in0=s, in1=banks[eb][:], op=ALU.mult), reads=[sB, bbuf[eb]], writes=[sB])
                    xt, xb = xrot.next()
                    Sx.add('sp', 'dma_start', dict(out=xt, in_=xsrc[fc * 128:(fc + 1) * 128, t0:t0 + TB]), writes=[xb], dma=True)
                    so, soB = so_r.next()
                    Sx.add('dve', 'tensor_tensor', dict(out=so, in0=xt, in1=s, op=ALU.add), reads=[xb, sB], writes=[soB])
                    Sx.add('pool', 'dma_start', dict(out=xdst[fc * 128:(fc + 1) * 128, t0:t0 + TB], in_=so), reads=[soB], dma=True)
            Sx.barrier()

        def phase_F(xsrc, do_norm):
            A = Arena(PBASE)
            xrot = Rot([A.f32(TB) for _ in range(4)])
            sqrot = Rot([A.bf16(TB) for _ in range(2)])
            rstd, rb = A.f32(TB), Buf()
            yt_r = Rot([A.f32(TB) for _ in range(2)])
            OUT = [A.f32(D) for _ in range(TB // 128)]
            OB = [Buf() for _ in range(TB // 128)]
            for tb in range(NTB):
                t0 = tb * TB
                if do_norm:
                    bi = next_bank()
                    ss, ssb = banks[bi], bbuf[bi]
                    for c in range(KC):
                        xt, xb = xrot.next()
                        Sx.add('sp', 'dma_start', dict(out=xt, in_=xsrc[c * 128:(c + 1) * 128, t0:t0 + TB]), writes=[xb], dma=True)
                        sq, sqb = sqrot.next()
                        Sx.add('act', 'activation', dict(out=sq, in_=xt, func=AF.Square), reads=[xb], writes=[sqb])
                        Sx.add('pe', 'matmul', dict(out=ss[:], lhsT=ones_b, rhs=sq, start=(c == 0), stop=(c == KC - 1)), reads=[sqb, cb], writes=[ssb])
                    Sx.add('act', 'activation', dict(out=rstd, in_=ss[:], func=AF.Sqrt, bias=epsc, scale=1.0 / D), reads=[ssb, cb], writes=[rb])
                    Sx.add('dve', 'reciprocal', dict(out=rstd, in_=rstd), reads=[rb], writes=[rb])
                for c in range(KC):
                    xt, xb = xrot.next()
                    Sx.add('sp', 'dma_start', dict(out=xt, in_=xsrc[c * 128:(c + 1) * 128, t0:t0 + TB]), writes=[xb], dma=True)
                    if do_norm:
                        yt, ytB = yt_r.next()
                        Sx.add('dve', 'scalar_tensor_tensor', dict(out=yt, in0=xt, scalar=vcol(0, 'g_fin', c), in1=rstd, op0=ALU.mult, op1=ALU.mult), reads=[xb, rb, cb], writes=[ytB])
                    else:
                        yt, ytB = xt, xb
                    b2 = next_bank()
                    for tt in range(TB // 128):
                        Sx.add('pe', 'transpose', dict(out=banks[b2][:, tt * 128:(tt + 1) * 128], in_=yt[:, tt * 128:(tt + 1) * 128], identity=ident_f), reads=[ytB, cb], writes=[bbuf[b2]])
                    for tt in range(TB // 128):
                        Sx.add('act', 'activation', dict(out=OUT[tt][:, c * 128:(c + 1) * 128], in_=banks[b2][:, tt * 128:(tt + 1) * 128], func=AF.Copy), reads=[bbuf[b2]], writes=[OB[tt]])
                for tt in range(TB // 128):
                    Sx.add('pool', 'dma_start', dict(out=y_out[t0 + tt * 128:t0 + (tt + 1) * 128, :], in_=OUT[tt]), reads=[OB[tt]], dma=True)
            Sx.barrier()

        phase_input()
        cur = 0
        for l in range(nlayers):
            phase_A(l, XT[cur])
            phase_RG(l)
            phase_ATT(l)
            phase_C(l, XT[cur], XT[1 - cur])
            cur = 1 - cur
            NJ_ = DFF // 128
            _, KSd_, KCd_ = cfg.slab_geom('w_down')
            if KSd_ >= 2:
                jm = (KSd_ // 2) * KCd_
                phase_D(l, XT[cur], XT[cur], XT[2], 0, jm)
                phase_D(l, XT[cur], XT[2], XT[1 - cur], jm, NJ_)
            else:
                phase_D(l, XT[cur], XT[cur], XT[1 - cur], 0, NJ_)
            cur = 1 - cur
            phase_E(l, XT[cur], XT[1 - cur])
            cur = 1 - cur
        phase_F(XT[cur], do_norm)
        Sx.finish()
        Sx.emit(st)
    return nc


def prep_weights(cfg, inp, nlayers):
    out = {}
    vec = np.zeros((128, cfg.depth * cfg.NV), np.float32)

    def put(o, nm, arr):
        a = np.asarray(arr, np.float32)
        vec[:, o + cfg.voff[nm]:o + cfg.voff[nm] + a.shape[1]] = a
    put(0, 'g_fin', colvec(inp['g_final']))
    for l in range(nlayers):
        mats = {
            'w_in': inp['w_in'][l],
            'w_br': inp['w_branch'][l].reshape(cfg.RGW + cfg.SBW, cfg.D),
            'w_bg': inp['w_branch_gate'][l], 'w_out': inp['w_out'][l], 'w_up': inp['w_up'][l],
            'w_down': inp['w_down'][l], 'w_pg': inp['w_ple_gate'][l], 'w_ple': inp['w_ple'][l],
            'w_rga': inp['w_rg_a'][l].transpose(1, 0, 2).reshape(128, cfg.RGW),
            'w_rgx': inp['w_rg_x'][l].transpose(1, 0, 2).reshape(128, cfg.RGW),
        }
        for nm, W in mats.items():
            NF, KS, KCs = cfg.slab_geom(nm)
            out["%s_%d" % (nm, l)] = slabify(np.asarray(W, np.float32), KS, KCs)
        o = l * cfg.NV
        put(o, 'g_mix', colvec(inp['g_mix'][l]))
        put(o, 'g_ffn', colvec(inp['g_ffn'][l]))
        put(o, 'g_ple', colvec(inp['g_ple'][l]))
        put(o, 'b_bg', colvec(inp['b_branch_gate'][l]))
        put(o, 'b_pg', colvec(inp['b_ple_gate'][l]))
        wc = np.asarray(inp['w_rg_conv'][l])
        put(o, 'w_rgc', wc.reshape(4, cfg.NRB, 128).transpose(2, 1, 0).reshape(128, cfg.NRB * 4))
        put(o, 'b_rgc', colvec(inp['b_rg_conv'][l]))
        put(o, 'b_rga', colvec(inp['b_rg_a'][l]))
        put(o, 'b_rgx', colvec(inp['b_rg_x'][l]))
        put(o, 'lam', colvec(inp['rg_lambda'][l]))
        wfc = np.asarray(inp['w_ffn_conv'][l])
        put(o, 'w_fc', wfc.reshape(3, cfg.NFF, 128).transpose(2, 1, 0).reshape(128, cfg.NFF * 3))
        put(o, 'b_fc', colvec(inp['b_ffn_conv'][l]))
    out['vec'] = vec
    return out


def run(cfg, inp, n_cores, debug_taps=(), nlayers=None, do_norm=True):
    if nlayers is None:
        nlayers = cfg.depth
    B = inp['x'].shape[0]
    nc = build_program(cfg, debug_taps, nlayers, do_norm)
    shared = prep_weights(cfg, inp, nlayers)
    in_maps = []
    for c in range(n_cores):
        b = c % B
        m = dict(shared)
        m['x'] = np.ascontiguousarray(np.asarray(inp['x'][b], np.float32))
        m['p'] = np.ascontiguousarray(np.asarray(inp['p'][:cfg.depth, b], np.float32))
        in_maps.append(m)
    res = run_bass_kernel_spmd(nc, in_maps, core_ids=list(range(n_cores)))
    return res


LAYER_KEYS = ['p', 'g_mix', 'w_in', 'w_rg_conv', 'b_rg_conv', 'w_rg_a', 'b_rg_a', 'w_rg_x', 'b_rg_x', 'rg_lambda',
              'w_branch', 'w_branch_gate', 'b_branch_gate', 'w_out', 'g_ffn', 'w_up', 'w_ffn_conv', 'b_ffn_conv',
              'w_down', 'g_ple', 'w_ple', 'w_ple_gate', 'b_ple_gate']


def kernel_unfused(inp, cfg1, depth, n_cores):
    B = inp['x'].shape[0]
    x = inp['x']
    for l in range(depth):
        sub = {k: inp[k][l:l + 1] for k in LAYER_KEYS}
        sub['x'] = x
        sub['g_final'] = inp['g_final']
        res = run(cfg1, sub, n_cores, nlayers=1, do_norm=False)
        x = np.stack([np.asarray(res.results[b]["y"], np.float32) for b in range(B)], axis=0)
    sub = {'x': x, 'p': inp['p'][0:1], 'g_final': inp['g_final']}
    res = run(cfg1, sub, n_cores, nlayers=0, do_norm=True)
    return np.stack([np.asarray(res.results[b]["y"], np.float32) for b in range(B)], axis=0)


def kernel(**inputs):
    inp = {k: np.asarray(v) for k, v in inputs.items()}
    depth = inp['w_in'].shape[0]
    return kernel_unfused(inp, Cfg(depth=1), depth, 4)
```

```python
import numpy as np
from contextlib import ExitStack
import concourse.bass as bass
import concourse.mybir as mybir
from concourse.bass_utils import run_bass_kernel_spmd

F32 = mybir.dt.float32
BF16 = mybir.dt.bfloat16
I32 = mybir.dt.int32
AF = mybir.ActivationFunctionType
ALU = mybir.AluOpType

ENGS = ['pe', 'act', 'dve', 'pool', 'sp']
NS = {'sp': 16, 'act': 8, 'pool': 8, 'pe': 0, 'dve': 0}
SAME_ENGINE_SYNC = False
EMBED_WAITS = True
EPS = 1e-6


class Buf:
    __slots__ = ('name', 'w', 'r')

    def __init__(self, name=''):
        self.name = name
        self.w = None
        self.r = []


class Sched:
    def __init__(self, nc):
        self.nc = nc
        self.ins = {e: [] for e in ENGS}
        self.ndma = {e: 0 for e in ENGS}
        self.last_real = {e: None for e in ENGS}

    def add(self, eng, name, kw, reads=(), writes=(), dma=False):
        fn = (name, kw)
        deps = set()
        for b in reads:
            if b.w is not None:
                deps.add(b.w)
        for b in writes:
            if b.w is not None:
                deps.add(b.w)
            deps.update(b.r)
        idx = len(self.ins[eng])
        j = None
        if dma:
            j = self.ndma[eng]
            self.ndma[eng] += 1
            ev = ('d', eng, j)
            if j >= NS[eng]:
                deps.add(('d', eng, j - NS[eng]))
        else:
            ev = ('c', eng, idx)
            self.last_real[eng] = idx
            if eng == 'pe' or (not SAME_ENGINE_SYNC and eng != 'pool'):
                deps = {d for d in deps if not (d[0] == 'c' and d[1] == eng)}
        self.ins[eng].append([fn, deps, ev, False, j])
        for b in reads:
            if ev[0] == 'c':
                b.r = [e for e in b.r if not (e[0] == 'c' and e[1] == ev[1])]
            b.r.append(ev)
        for b in writes:
            b.w = ev
            b.r = []
        return ev

    def _all_pending(self):
        deps = set()
        for q in ENGS:
            n = self.ndma[q]
            for j in range(max(0, n - NS[q]), n):
                deps.add(('d', q, j))
        for e in ENGS:
            if self.last_real[e] is not None:
                deps.add(('c', e, self.last_real[e]))
        return deps

    def barrier(self):
        deps = self._all_pending()
        for e in ENGS:
            self.ins[e].append([None, set(deps), None, False, None])

    def finish(self):
        self.barrier()

    def emit(self, stack):
        nc = self.nc
        csem = {e: stack.enter_context(nc.semaphore('c_' + e)) for e in ENGS}
        dsem = {e: [stack.enter_context(nc.semaphore('d_%s_%d' % (e, i))) for i in range(NS[e])]
                for e in ENGS}
        for e in ENGS:
            for rec in self.ins[e]:
                for d in rec[1]:
                    if d[0] == 'c':
                        self.ins[d[1]][d[2]][3] = True
        val = {}
        for e in ENGS:
            c = 0
            v = []
            for rec in self.ins[e]:
                if rec[3]:
                    c += 1
                v.append(c)
            val[e] = v

        def run(eng, engine):
            waited = {}
            for fn, deps, ev, sig, j in self.ins[eng]:
                need = {}
                for d in deps:
                    if d[0] == 'c':
                        key = ('c', d[1])
                        sem = csem[d[1]]
                        v = val[d[1]][d[2]]
                    else:
                        q, jj = d[1], d[2]
                        key = ('d', q, jj % NS[q])
                        sem = dsem[q][jj % NS[q]]
                        v = 16 * (jj // NS[q] + 1)
                    if need.get(key, (None, 0))[1] < v:
                        need[key] = (sem, v)
                todo = []
                for key, (sem, v) in need.items():
                    if waited.get(key, 0) < v:
                        todo.append((sem, v))
                        waited[key] = v
                emb = todo.pop() if (todo and fn is not None and EMBED_WAITS) else None
                for sem, v in todo:
                    engine.wait_ge(sem, v)
                if fn is None:
                    continue
                kw = {k: (v(engine) if callable(v) else v) for k, v in fn[1].items()}
                inst = getattr(engine, fn[0])(**kw)
                if emb is not None:
                    inst._wait_ge(emb[0], emb[1])
                if ev[0] == 'd':
                    inst.then_inc(dsem[eng][j % NS[eng]], 16)
                elif sig:
                    inst.then_inc(csem[eng], 1)

        with nc.Block() as block:
            @block.sync
            def _(sync):
                run('sp', sync)

            @block.scalar
            def _(scalar):
                run('act', scalar)

            @block.vector
            def _(vector):
                run('dve', vector)

            @block.gpsimd
            def _(gpsimd):
                run('pool', gpsimd)

            @block.tensor
            def _(tensor):
                run('pe', tensor)


class Cfg:
    def __init__(self, D=4096, S=4096, depth=2, RGW=2048, SBW=2048, DFF=12288, PLE=256):
        self.D, self.S, self.depth = D, S, depth
        self.RGW, self.SBW, self.DFF, self.PLE = RGW, SBW, DFF, PLE
        self.KC = D // 128
        self.NRB = RGW // 128
        self.NH = SBW // 128
        self.WIN = 2 * RGW + 3 * SBW
        self.NFF = 2 * DFF // 128
        self.TB = 512
        self.NTB = S // 512
        self.wshapes = {
            'w_in': (D, self.WIN), 'w_br': (RGW + SBW, D), 'w_bg': (D, 2 * D), 'w_out': (D, D),
            'w_up': (D, 2 * DFF), 'w_down': (DFF, D), 'w_pg': (D, D), 'w_ple': (PLE, D),
            'w_rga': (128, RGW), 'w_rgx': (128, RGW),
        }
        off = {}
        c = 0
        for nm, n in [('g_mix', self.KC), ('g_ffn', self.KC), ('g_ple', self.KC), ('g_fin', self.KC),
                      ('b_bg', 2 * self.KC), ('b_pg', self.KC), ('w_rgc', 4 * self.NRB), ('b_rgc', self.NRB),
                      ('b_rga', self.NRB), ('b_rgx', self.NRB), ('lam', self.NRB),
                      ('w_fc', 3 * self.NFF), ('b_fc', self.NFF)]:
            off[nm] = c
            c += n
        self.voff = off
        self.NV = c

    def slab_geom(self, nm):
        K, N = self.wshapes[nm]
        kc = K // 128
        KS = (kc + 31) // 32
        KCs = kc // KS
        if nm == 'w_down' and kc > 32 and kc % 16 == 0:
            KCs = 16
            KS = kc // 16
        return N // 128, KS, KCs


def slabify(W, KS, KCs):
    K, N = W.shape
    NF = N // 128
    a = W.reshape(KS, KCs, 128, NF, 128).transpose(3, 0, 2, 1, 4)
    return np.ascontiguousarray(a).reshape(NF * KS * 128, KCs * 128)


def colvec(v):
    return np.ascontiguousarray(v.reshape(-1, 128).T)


def build_program(cfg, debug_taps=(), nlayers=None, do_norm=True):
    if nlayers is None:
        nlayers = cfg.depth
    D, S, KC, TB, NTB = cfg.D, cfg.S, cfg.KC, cfg.TB, cfg.NTB
    NRB, NH, NFF, RGW, SBW, DFF = cfg.NRB, cfg.NH, cfg.NFF, cfg.RGW, cfg.SBW, cfg.DFF
    nc = bass.Bass("TRN2", target_bir_lowering=False)
    x_in = nc.dram_tensor("x", [S, D], F32, kind="ExternalInput").ap()
    p_in = nc.dram_tensor("p", [cfg.depth, S, cfg.PLE], F32, kind="ExternalInput").ap()
    y_out = nc.dram_tensor("y", [S, D], F32, kind="ExternalOutput").ap()
    vec_in = nc.dram_tensor("vec", [128, cfg.depth * cfg.NV], F32, kind="ExternalInput").ap()
    wf, wb = {}, {}
    for l in range(nlayers):
        for nm in cfg.wshapes:
            NF, KS, KCs = cfg.slab_geom(nm)
            shp = [NF * KS * 128, KCs * 128]
            wf[l, nm] = nc.dram_tensor("%s_%d" % (nm, l), shp, F32, kind="ExternalInput").ap()
            wb[l, nm] = nc.dram_tensor("b%s_%d" % (nm, l), shp, BF16, kind="Internal").ap()

    def scratch(nm, shp, dt):
        kind = "ExternalOutput" if nm in debug_taps else "Internal"
        return nc.dram_tensor(nm, shp, dt, kind=kind).ap()
    XT = [scratch("XT0", [D, S], F32), scratch("XT1", [D, S], F32), scratch("XT2", [D, S], F32)]
    PXG = scratch("PXG", [2 * RGW, S], F32)
    PQKV = scratch("PQKV", [3 * SBW, S], BF16)
    YBR = scratch("YBR", [RGW + SBW, S], BF16)

    with ExitStack() as st:
        Sx = Sched(nc)
        AW = 44 * 1024
        arena = st.enter_context(nc.sbuf_tensor("arena", [128, AW], F32))
        banks = [st.enter_context(nc.psum_tensor("bank%d" % i, [128, 512], F32)) for i in range(8)]
        bbuf = [Buf("bank%d" % i) for i in range(8)]

        class Arena:
            def __init__(self, base):
                self.base = base
                self.cur = base

            def f32(self, n):
                a = arena[:, self.cur:self.cur + n]
                self.cur += n
                assert self.cur <= AW, self.cur
                return a

            def bf16(self, n):
                w = (n + 1) // 2
                a = arena[:, self.cur:self.cur + w].bitcast(BF16)
                self.cur += w
                assert self.cur <= AW, self.cur
                return a

            def i32(self, n):
                a = arena[:, self.cur:self.cur + n].bitcast(I32)
                self.cur += n
                return a

        P = Arena(0)
        vec = P.f32(cfg.depth * cfg.NV)
        ident_f = P.f32(128)
        ident_b = P.bf16(128)
        ones_b = P.bf16(128)
        L1 = P.bf16(128)
        L2 = P.bf16(128)
        maskb = P.bf16(4 * 512)
        negmb = P.bf16(4 * 512)
        epsc = P.f32(1)
        onec = P.f32(1)
        rgsc = P.f32(cfg.depth * NRB)
        tmpi = P.i32(512)
        tmpf = P.f32(512)
        cb = Buf('const')
        Sx.add('sp', 'dma_start', dict(out=vec, in_=vec_in), writes=[cb], dma=True)
        Sx.add('dve', 'memset', dict(ap=epsc, constant=EPS), writes=[cb])
        Sx.add('dve', 'memset', dict(ap=onec, constant=1.0), writes=[cb])
        Sx.add('dve', 'memset', dict(ap=ones_b, constant=1.0), writes=[cb])
        tb_ = Buf('tmp')
        Sx.add('pool', 'iota', dict(out=tmpi[:, 0:128], pattern=[[1, 128]], base=0, channel_multiplier=-1), writes=[tb_])
        Sx.add('dve', 'tensor_single_scalar', dict(out=ident_f, in_=tmpi[:, 0:128], scalar=0, op=ALU.is_equal), reads=[tb_], writes=[cb])
        Sx.add('dve', 'tensor_single_scalar', dict(out=ident_b, in_=tmpi[:, 0:128], scalar=0, op=ALU.is_equal), reads=[tb_], writes=[cb])
        Sx.add('dve', 'tensor_single_scalar', dict(out=L1, in_=tmpi[:, 0:128], scalar=0, op=ALU.is_lt), reads=[tb_], writes=[cb])
        Sx.add('dve', 'tensor_single_scalar', dict(out=L2, in_=tmpi[:, 0:128], scalar=0, op=ALU.is_ge), reads=[tb_], writes=[cb])
        for j in range(4):
            Sx.add('pool', 'iota', dict(out=tmpi, pattern=[[1, 512]], base=-128 * j, channel_multiplier=-1), reads=[cb], writes=[tb_])
            Sx.add('dve', 'tensor_single_scalar', dict(out=maskb[:, j * 512:(j + 1) * 512], in_=tmpi, scalar=0, op=ALU.is_gt), reads=[tb_], writes=[cb])
            Sx.add('dve', 'tensor_scalar', dict(out=negmb[:, j * 512:(j + 1) * 512], in0=maskb[:, j * 512:(j + 1) * 512], scalar1=-1.0, scalar2=3000.0, op0=ALU.add, op1=ALU.mult), reads=[cb], writes=[cb])
        for l in range(cfg.depth):
            lo = l * cfg.NV + cfg.voff['lam']
            Sx.add('act', 'activation', dict(out=rgsc[:, l * NRB:(l + 1) * NRB], in_=vec[:, lo:lo + NRB], func=AF.Exp, scale=-1.0), reads=[cb], writes=[cb])
            Sx.add('act', 'activation', dict(out=rgsc[:, l * NRB:(l + 1) * NRB], in_=rgsc[:, l * NRB:(l + 1) * NRB], func=AF.Ln, bias=onec, scale=1.0), reads=[cb], writes=[cb])
            Sx.add('dve', 'tensor_scalar', dict(out=rgsc[:, l * NRB:(l + 1) * NRB], in0=rgsc[:, l * NRB:(l + 1) * NRB], scalar1=-8.0, scalar2=None, op0=ALU.mult), reads=[cb], writes=[cb])
        PBASE = P.cur

        for l in range(nlayers):
            for nm in cfg.wshapes:
                src, dst = wf[l, nm], wb[l, nm]
                R, L = src.shape
                if L > 2048:
                    src = src.rearrange("r (a b) -> (r a) b", b=2048)
                    dst = dst.rearrange("r (a b) -> (r a) b", b=2048)
                    R = R * (L // 2048)
                for r0 in range(0, R, 8192):
                    r1 = min(R, r0 + 8192)
                    Sx.add('pool', 'dma_start', dict(out=dst[r0:r1, :], in_=src[r0:r1, :]), dma=True)
        Sx.barrier()

        bank_rr = [0]

        def next_bank():
            i = bank_rr[0]
            bank_rr[0] = (i + 1) % 8
            return i

        class Rot:
            def __init__(self, tiles):
                self.t = tiles
                self.b = [Buf() for _ in tiles]
                self.i = 0

            def next(self):
                k = self.i
                self.i = (k + 1) % len(self.t)
                return self.t[k], self.b[k]

        def vcol(l, nm, c):
            o = l * cfg.NV + cfg.voff[nm] + c
            return vec[:, o:o + 1]

        def load_slab(l, nm, fc, ks, wrot):
            NF, KS, KCs = cfg.slab_geom(nm)
            t, b = wrot.next()
            r0 = (fc * KS + ks) * 128
            src = wb[l, nm][r0:r0 + 128, :]
            Sx.add('sp', 'dma_start', dict(out=t[:, 0:KCs * 128], in_=src), writes=[b], dma=True)
            return t, b, KCs

        def norm_pass(A, xsrc, t0, l, gname, hT, hb, xrot, sqrot):
            bi = next_bank()
            ss, ssb = banks[bi], bbuf[bi]
            for c in range(KC):
                xt, xb = xrot.next()
                Sx.add('sp', 'dma_start', dict(out=xt, in_=xsrc[c * 128:(c + 1) * 128, t0:t0 + TB]), writes=[xb], dma=True)
                sq, sqb = sqrot.next()
                Sx.add('act', 'activation', dict(out=sq, in_=xt, func=AF.Square), reads=[xb], writes=[sqb])
                Sx.add('pe', 'matmul', dict(out=ss[:], lhsT=ones_b, rhs=sq, start=(c == 0), stop=(c == KC - 1)), reads=[sqb, cb], writes=[ssb])
            rstd, rb = A['rstd']
            Sx.add('act', 'activation', dict(out=rstd, in_=ss[:], func=AF.Sqrt, bias=epsc, scale=1.0 / D), reads=[ssb, cb], writes=[rb])
            Sx.add('dve', 'reciprocal', dict(out=rstd, in_=rstd), reads=[rb], writes=[rb])
            for c in range(KC):
                xt, xb = xrot.next()
                Sx.add('sp', 'dma_start', dict(out=xt, in_=xsrc[c * 128:(c + 1) * 128, t0:t0 + TB]), writes=[xb], dma=True)
                Sx.add('dve', 'scalar_tensor_tensor', dict(out=hT[:, c * TB:(c + 1) * TB], in0=xt, scalar=vcol(l, gname, c), in1=rstd, op0=ALU.mult, op1=ALU.mult), reads=[xb, rb, cb], writes=[hb])

        def mm_group(bi, specs, rhs_fn, rhs_bufs):
            n = sum(cfg.slab_geom(sp_[1])[2] for sp_ in specs)
            i = 0
            for sp_ in specs:
                t, b, KCs = load_slab(*sp_)
                for kc in range(KCs):
                    Sx.add('pe', 'matmul', dict(out=banks[bi][:], lhsT=t[:, kc * 128:(kc + 1) * 128], rhs=rhs_fn(i), start=(i == 0), stop=(i == n - 1)),
                           reads=[b] + rhs_bufs, writes=[bbuf[bi]])
                    i += 1

        def phase_input():
            A = Arena(PBASE)
            xin = Rot([A.f32(D) for _ in range(2)])
            stg = Rot([A.f32(512) for _ in range(4)])
            for tt in range(S // 128):
                xt, xb = xin.next()
                Sx.add('sp', 'dma_start', dict(out=xt, in_=x_in[tt * 128:(tt + 1) * 128, :]), writes=[xb], dma=True)
                for c4 in range(KC // 4):
                    bi = next_bank()
                    for k in range(4):
                        c = c4 * 4 + k
                        Sx.add('pe', 'transpose', dict(out=banks[bi][:, k * 128:(k + 1) * 128], in_=xt[:, c * 128:(c + 1) * 128], identity=ident_f), reads=[xb, cb], writes=[bbuf[bi]])
                    sg, sb = stg.next()
                    Sx.add('act', 'activation', dict(out=sg, in_=banks[bi][:], func=AF.Copy), reads=[bbuf[bi]], writes=[sb])
                    dst = XT[0][c4 * 512:(c4 + 1) * 512, tt * 128:(tt + 1) * 128].rearrange("(k p) t -> p k t", p=128)
                    Sx.add('pool', 'dma_start', dict(out=dst, in_=sg.rearrange("p (k t) -> p k t", t=128)), reads=[sb], dma=True)
            Sx.barrier()

        def phase_A(l, xsrc):
            A = Arena(PBASE)
            hT = A.bf16(KC * TB)
            hb = Buf('hT')
            xrot = Rot([A.f32(TB) for _ in range(4)])
            sqrot = Rot([A.bf16(TB) for _ in range(2)])
            AA = {'rstd': (A.f32(TB), Buf())}
            wrot = Rot([A.bf16(32 * 128) for _ in range(4)])
            sf = Rot([A.f32(TB) for _ in range(3)])
            sbh = Rot([A.bf16(TB) for _ in range(3)])
            for tb in range(NTB):
                t0 = tb * TB
                norm_pass(AA, xsrc, t0, l, 'g_mix', hT, hb, xrot, sqrot)
                for fc in range(cfg.WIN // 128):
                    bi = next_bank()
                    mm_group(bi, [(l, 'w_in', fc, 0, wrot)], lambda kc: hT[:, kc * TB:(kc + 1) * TB], [hb])
                    if fc < 2 * NRB:
                        sg, sb = sf.next()
                        dst = PXG[fc * 128:(fc + 1) * 128, t0:t0 + TB]
                    else:
                        sg, sb = sbh.next()
                        f2 = fc - 2 * NRB
                        dst = PQKV[f2 * 128:(f2 + 1) * 128, t0:t0 + TB]
                    Sx.add('act', 'activation', dict(out=sg, in_=banks[bi][:], func=AF.Copy), reads=[bbuf[bi]], writes=[sb])
                    Sx.add('pool', 'dma_start', dict(out=dst, in_=sg), reads=[sb], dma=True)
            Sx.barrier()

        def phase_RG(l):
            A = Arena(PBASE)
            TC = min(1024, S)
            NTC = S // TC
            xr_r = Rot([A.f32(TC + 3) for _ in range(2)])
            gr_r = Rot([A.f32(TC) for _ in range(2)])
            xc, xcB = A.f32(TC), Buf()
            xcb, xcbB = A.bf16(TC), Buf()
            rg, rgB = A.f32(TC), Buf()
            ig, igB = A.f32(TC), Buf()
            av, avB = A.f32(TC), Buf()
            mu, muB = A.f32(TC), Buf()
            hh_r = Rot([A.f32(TC) for _ in range(2)])
            ge, geB = A.f32(TC), Buf()
            yo_r = Rot([A.bf16(TC) for _ in range(2)])
            wa_r = Rot([A.bf16(128) for _ in range(2)])
            wx_r = Rot([A.bf16(128) for _ in range(2)])
            for cbk in range(NRB):
                wa, waB, _ = load_slab(l, 'w_rga', cbk, 0, wa_r)
                wx, wxB, _ = load_slab(l, 'w_rgx', cbk, 0, wx_r)
                prev_h = None
                for tcn in range(NTC):
                    t0 = tcn * TC
                    xr, xrB = xr_r.next()
                    gr, grB = gr_r.next()
                    if tcn == 0:
                        Sx.add('dve', 'memset', dict(ap=xr[:, 0:3], constant=0.0), writes=[xrB])
                        Sx.add('sp', 'dma_start', dict(out=xr[:, 3:3 + TC], in_=PXG[cbk * 128:(cbk + 1) * 128, 0:TC]), writes=[xrB], dma=True)
                    else:
                        Sx.add('sp', 'dma_start', dict(out=xr[:, 0:3 + TC], in_=PXG[cbk * 128:(cbk + 1) * 128, t0 - 3:t0 + TC]), writes=[xrB], dma=True)
                    Sx.add('sp', 'dma_start', dict(out=gr, in_=PXG[RGW + cbk * 128:RGW + (cbk + 1) * 128, t0:t0 + TC]), writes=[grB], dma=True)
                    Sx.add('dve', 'tensor_scalar', dict(out=xc, in0=xr[:, 3:3 + TC], scalar1=vcol(l, 'w_rgc', cbk * 4 + 3), scalar2=vcol(l, 'b_rgc', cbk), op0=ALU.mult, op1=ALU.add), reads=[xrB, cb], writes=[xcB])
                    for k in range(3):
                        Sx.add('dve', 'scalar_tensor_tensor', dict(out=xc, in0=xr[:, k:k + TC], scalar=vcol(l, 'w_rgc', cbk * 4 + k), in1=xc, op0=ALU.mult, op1=ALU.add), reads=[xrB, cb, xcB], writes=[xcB])
                    Sx.add('pool', 'tensor_copy', dict(out=xcb, in_=xc), reads=[xcB], writes=[xcbB])
                    for sub in range(TC // 512):
                        sl = slice(sub * 512, (sub + 1) * 512)
                        b1 = next_bank()
                        Sx.add('pe', 'matmul', dict(out=banks[b1][:], lhsT=wa, rhs=xcb[:, sl], start=True, stop=True), reads=[waB, xcbB], writes=[bbuf[b1]])
                        Sx.add('act', 'activation', dict(out=rg[:, sl], in_=banks[b1][:], func=AF.Sigmoid, bias=vcol(l, 'b_rga', cbk), scale=1.0), reads=[bbuf[b1], cb], writes=[rgB])
                        b2 = next_bank()
                        Sx.add('pe', 'matmul', dict(out=banks[b2][:], lhsT=wx, rhs=xcb[:, sl], start=True, stop=True), reads=[wxB, xcbB], writes=[bbuf[b2]])
                        Sx.add('act', 'activation', dict(out=ig[:, sl], in_=banks[b2][:], func=AF.Sigmoid, bias=vcol(l, 'b_rgx', cbk), scale=1.0), reads=[bbuf[b2], cb], writes=[igB])
                    Sx.add('act', 'activation', dict(out=av, in_=rg, func=AF.Exp, scale=rgsc[:, l * NRB + cbk:l * NRB + cbk + 1]), reads=[rgB, cb], writes=[avB])
                    Sx.add('dve', 'tensor_tensor', dict(out=mu, in0=av, in1=av, op=ALU.mult), reads=[avB], writes=[muB])
                    Sx.add('act', 'activation', dict(out=mu, in_=mu, func=AF.Sqrt, bias=onec, scale=-1.0), reads=[muB, cb], writes=[muB])
                    Sx.add('dve', 'tensor_tensor', dict(out=ig, in0=ig, in1=xc, op=ALU.mult), reads=[igB, xcB], writes=[igB])
                    Sx.add('dve', 'tensor_tensor', dict(out=ig, in0=ig, in1=mu, op=ALU.mult), reads=[igB, muB], writes=[igB])
                    hh, hhB = hh_r.next()
                    if prev_h is None:
                        Sx.add('dve', 'tensor_tensor_scan', dict(out=hh, data0=av, data1=ig, initial=0.0, op0=ALU.mult, op1=ALU.add), reads=[avB, igB], writes=[hhB])
                    else:
                        ph, phB = prev_h
                        Sx.add('dve', 'tensor_tensor_scan', dict(out=hh, data0=av, data1=ig, initial=ph[:, TC - 1:TC], op0=ALU.mult, op1=ALU.add), reads=[avB, igB, phB], writes=[hhB])
                    prev_h = (hh, hhB)
                    Sx.add('act', 'activation', dict(out=ge, in_=gr, func=AF.Gelu_apprx_tanh), reads=[grB], writes=[geB])
                    yo, yoB = yo_r.next()
                    Sx.add('dve', 'tensor_tensor', dict(out=yo, in0=hh, in1=ge, op=ALU.mult), reads=[hhB, geB], writes=[yoB])
                    Sx.add('pool', 'dma_start', dict(out=YBR[cbk * 128:(cbk + 1) * 128, t0:t0 + TC], in_=yo), reads=[yoB], dma=True)
            Sx.barrier()

        def phase_ATT(l):
            A = Arena(PBASE)
            scale = 128.0 ** -0.5
            qT, qB = A.bf16(S), Buf()
            kT, kB = A.bf16(S), Buf()
            vT, vTB = A.bf16(S), Buf()
            vv, vB = A.bf16(S), Buf()
            e_r = Rot([A.f32(512) for _ in range(2)])
            sp_r = Rot([A.f32(512) for _ in range(2)])
            spb_r = Rot([A.bf16(512) for _ in range(2)])
            t_r = Rot([A.f32(512) for _ in range(2)])
            a_r = Rot([A.bf16(512) for _ in range(2)])
            yo_r = Rot([A.bf16(512) for _ in range(2)])
            NKB = S // 128
            for hd in range(NH):
                Sx.add('sp', 'dma_start', dict(out=qT, in_=PQKV[hd * 128:(hd + 1) * 128, :]), writes=[qB], dma=True)
                Sx.add('sp', 'dma_start', dict(out=kT, in_=PQKV[SBW + hd * 128:SBW + (hd + 1) * 128, :]), writes=[kB], dma=True)
                Sx.add('sp', 'dma_start', dict(out=vT, in_=PQKV[2 * SBW + hd * 128:2 * SBW + (hd + 1) * 128, :]), writes=[vTB], dma=True)
                for k8 in range(NKB // 8):
                    bi = next_bank()
                    pb = banks[bi][:].bitcast(BF16)
                    for k in range(8):
                        kb = k8 * 8 + k
                        Sx.add('pe', 'transpose', dict(out=pb[:, k * 128:(k + 1) * 128], in_=vT[:, kb * 128:(kb + 1) * 128], identity=ident_b), reads=[vTB, cb], writes=[bbuf[bi]])
                    Sx.add('act', 'activation', dict(out=vv[:, k8 * 1024:(k8 + 1) * 1024], in_=pb[:, 0:1024], func=AF.Copy), reads=[bbuf[bi]], writes=[vB])
                for qg in range(S // 512):
                    q0 = qg * 512
                    ysb = next_bank()
                    sab = next_bank()
                    kbs = list(range(4 * qg + 3, -1, -1))
                    for n, kb in enumerate(kbs):
                        first, last = (n == 0), (n == len(kbs) - 1)
                        jd = kb - 4 * qg
                        zb = next_bank()
                        while zb in (ysb, sab):
                            zb = next_bank()
                        Sx.add('pe', 'matmul', dict(out=banks[zb][:], lhsT=kT[:, kb * 128:(kb + 1) * 128], rhs=qT[:, q0:q0 + 512], start=True, stop=True), reads=[kB, qB], writes=[bbuf[zb]])
                        ee, eB = e_r.next()
                        sp, spB = sp_r.next()
                        spb, spbB = spb_r.next()
                        Sx.add('act', 'activation', dict(out=ee, in_=banks[zb][:], func=AF.Exp, scale=scale), reads=[bbuf[zb]], writes=[eB])
                        Sx.add('act', 'activation', dict(out=sp, in_=ee, func=AF.Ln, bias=onec, scale=1.0), reads=[eB, cb], writes=[spB])
                        if jd >= 0:
                            Sx.add('dve', 'tensor_tensor', dict(out=spb, in0=sp, in1=maskb[:, jd * 512:(jd + 1) * 512], op=ALU.mult), reads=[spB, cb], writes=[spbB])
                        else:
                            Sx.add('pool', 'tensor_copy', dict(out=spb, in_=sp), reads=[spB], writes=[spbB])
                        Sx.add('pe', 'matmul', dict(out=banks[sab][:], lhsT=L1, rhs=spb, start=first, stop=False), reads=[spbB, cb], writes=[bbuf[sab]])
                        tt_, tB = t_r.next()
                        Sx.add('dve', 'scalar_tensor_tensor', dict(out=tt_, in0=banks[zb][:], scalar=scale, in1=sp, op0=ALU.mult, op1=ALU.subtract), reads=[bbuf[zb], spB], writes=[tB])
                        Sx.add('dve', 'tensor_tensor', dict(out=tt_, in0=tt_, in1=banks[sab][:], op=ALU.subtract), reads=[tB, bbuf[sab]], writes=[tB])
                        Sx.add('pe', 'matmul', dict(out=banks[sab][:], lhsT=L2, rhs=spb, start=False, stop=last), reads=[spbB, cb], writes=[bbuf[sab]])
                        if jd >= 0:
                            Sx.add('dve', 'tensor_tensor', dict(out=tt_, in0=tt_, in1=negmb[:, jd * 512:(jd + 1) * 512], op=ALU.add), reads=[tB, cb], writes=[tB])
                        aa, aB = a_r.next()
                        Sx.add('act', 'activation', dict(out=aa, in_=tt_, func=AF.Exp), reads=[tB], writes=[aB])
                        Sx.add('pe', 'matmul', dict(out=banks[ysb][:], lhsT=vv[:, kb * 128:(kb + 1) * 128], rhs=aa, start=first, stop=last), reads=[vB, aB], writes=[bbuf[ysb]])
                    yo, yoB = yo_r.next()
                    Sx.add('act', 'activation', dict(out=yo, in_=banks[ysb][:], func=AF.Copy), reads=[bbuf[ysb]], writes=[yoB])
                    Sx.add('pool', 'dma_start', dict(out=YBR[RGW + hd * 128:RGW + (hd + 1) * 128, q0:q0 + 512], in_=yo), reads=[yoB], dma=True)
            Sx.barrier()

        def phase_C(l, xsrc, xdst):
            A = Arena(PBASE)
            hT, hb = A.bf16(KC * TB), Buf()
            NBC = (RGW + SBW) // 128
            yT, yb = A.bf16(NBC * TB), Buf()
            mT, mb = A.bf16(KC * TB), Buf()
            xrot = Rot([A.f32(TB) for _ in range(4)])
            sqrot = Rot([A.bf16(TB) for _ in range(2)])
            AA = {'rstd': (A.f32(TB), Buf())}
            wrot = Rot([A.bf16(32 * 128) for _ in range(4)])
            s0_r = Rot([A.f32(TB) for _ in range(2)])
            s1_r = Rot([A.f32(TB) for _ in range(2)])
            so_r = Rot([A.f32(TB) for _ in range(2)])
            NRC = RGW // 128
            for tb in range(NTB):
                t0 = tb * TB
                norm_pass(AA, xsrc, t0, l, 'g_mix', hT, hb, xrot, sqrot)
                src = YBR[:, t0:t0 + TB].rearrange("(k p) t -> p k t", p=128)
                Sx.add('sp', 'dma_start', dict(out=yT.rearrange("p (k t) -> p k t", t=TB), in_=src), writes=[yb], dma=True)
                for fc in range(KC):
                    sbr = load_slab(l, 'w_br', fc, 0, wrot)
                    t, b, KCs = sbr
                    bd0, bd1 = next_bank(), next_bank()
                    for n_, (bi, k0, k1) in enumerate([(bd0, 0, NRC), (bd1, NRC, NBC)]):
                        for kc in range(k0, k1):
                            Sx.add('pe', 'matmul', dict(out=banks[bi][:], lhsT=t[:, kc * 128:(kc + 1) * 128], rhs=yT[:, kc * TB:(kc + 1) * TB], start=(kc == k0), stop=(kc == k1 - 1)), reads=[b, yb], writes=[bbuf[bi]])
                    g0, g1 = next_bank(), next_bank()
                    mm_group(g0, [(l, 'w_bg', fc, 0, wrot)], lambda kc: hT[:, kc * TB:(kc + 1) * TB], [hb])
                    mm_group(g1, [(l, 'w_bg', KC + fc, 0, wrot)], lambda kc: hT[:, kc * TB:(kc + 1) * TB], [hb])
                    s0, s0B = s0_r.next()
                    s1, s1B = s1_r.next()
                    Sx.add('act', 'activation', dict(out=s0, in_=banks[g0][:], func=AF.Sigmoid, bias=vcol(l, 'b_bg', fc), scale=1.0), reads=[bbuf[g0], cb], writes=[s0B])
                    Sx.add('act', 'activation', dict(out=s1, in_=banks[g1][:], func=AF.Sigmoid, bias=vcol(l, 'b_bg', KC + fc), scale=1.0), reads=[bbuf[g1], cb], writes=[s1B])
                    Sx.add('dve', 'tensor_tensor', dict(out=s0, in0=s0, in1=banks[bd0][:], op=ALU.mult), reads=[s0B, bbuf[bd0]], writes=[s0B])
                    Sx.add('dve', 'tensor_tensor', dict(out=s1, in0=s1, in1=banks[bd1][:], op=ALU.mult), reads=[s1B, bbuf[bd1]], writes=[s1B])
                    Sx.add('dve', 'tensor_tensor', dict(out=mT[:, fc * TB:(fc + 1) * TB], in0=s0, in1=s1, op=ALU.add), reads=[s0B, s1B], writes=[mb])
                for fc in range(KC):
                    bi = next_bank()
                    mm_group(bi, [(l, 'w_out', fc, 0, wrot)], lambda kc: mT[:, kc * TB:(kc + 1) * TB], [mb])
                    xt, xb = xrot.next()
                    Sx.add('sp', 'dma_start', dict(out=xt, in_=xsrc[fc * 128:(fc + 1) * 128, t0:t0 + TB]), writes=[xb], dma=True)
                    so, soB = so_r.next()
                    Sx.add('dve', 'tensor_tensor', dict(out=so, in0=xt, in1=banks[bi][:], op=ALU.add), reads=[xb, bbuf[bi]], writes=[soB])
                    Sx.add('pool', 'dma_start', dict(out=xdst[fc * 128:(fc + 1) * 128, t0:t0 + TB], in_=so), reads=[soB], dma=True)
            Sx.barrier()

        def phase_D(l, xsrc, xres, xdst, j0, j1):
            A = Arena(PBASE)
            hT, hb = A.bf16(KC * TB), Buf()
            NJ = DFF // 128
            nj = j1 - j0
            aT, ab = A.bf16(nj * TB), Buf()
            xrot = Rot([A.f32(TB) for _ in range(4)])
            sqrot = Rot([A.bf16(TB) for _ in range(2)])
            AA = {'rstd': (A.f32(TB), Buf())}
            wrot = Rot([A.bf16(32 * 128) for _ in range(4)])
            HL, HLB = A.f32(NFF * 2), Buf()
            ua_r = Rot([A.f32(TB + 2) for _ in range(2)])
            uv_r = Rot([A.f32(TB + 2) for _ in range(2)])
            ca_r = Rot([A.f32(TB) for _ in range(2)])
            cv_r = Rot([A.f32(TB) for _ in range(2)])
            so_r = Rot([A.f32(TB) for _ in range(2)])
            Sx.add('dve', 'memset', dict(ap=HL, constant=0.0), writes=[HLB])
            _, KSd, KCd = cfg.slab_geom('w_down')
            assert j0 % KCd == 0 and j1 % KCd == 0
            for tb in range(NTB):
                t0 = tb * TB
                norm_pass(AA, xsrc, t0, l, 'g_ffn', hT, hb, xrot, sqrot)
                for j in range(j0, j1):
                    res = []
                    for half, rot_u, rot_c in [(0, ua_r, ca_r), (1, uv_r, cv_r)]:
                        f = half * NJ + j
                        bi = next_bank()
                        mm_group(bi, [(l, 'w_up', f, 0, wrot)], lambda kc: hT[:, kc * TB:(kc + 1) * TB], [hb])
                        u, uB = rot_u.next()
                        c_, cB = rot_c.next()
                        Sx.add('pool', 'tensor_copy', dict(out=u[:, 0:2], in_=HL[:, f * 2:f * 2 + 2]), reads=[HLB], writes=[uB])
                        Sx.add('act', 'activation', dict(out=u[:, 2:2 + TB], in_=banks[bi][:], func=AF.Copy), reads=[bbuf[bi]], writes=[uB])
                        Sx.add('pool', 'tensor_copy', dict(out=HL[:, f * 2:f * 2 + 2], in_=u[:, TB:TB + 2]), reads=[uB], writes=[HLB])
                        Sx.add('dve', 'tensor_scalar', dict(out=c_, in0=u[:, 2:2 + TB], scalar1=vcol(l, 'w_fc', f * 3 + 2), scalar2=vcol(l, 'b_fc', f), op0=ALU.mult, op1=ALU.add), reads=[uB, cb], writes=[cB])
                        for k in range(2):
                            Sx.add('dve', 'scalar_tensor_tensor', dict(out=c_, in0=u[:, k:k + TB], scalar=vcol(l, 'w_fc', f * 3 + k), in1=c_, op0=ALU.mult, op1=ALU.add), reads=[uB, cb, cB], writes=[cB])
                        res.append((c_, cB))
                    (ca, caB), (cv, cvB) = res
                    Sx.add('act', 'activation', dict(out=ca, in_=ca, func=AF.Gelu_apprx_tanh), reads=[caB], writes=[caB])
                    Sx.add('dve', 'tensor_tensor', dict(out=aT[:, (j - j0) * TB:(j - j0 + 1) * TB], in0=ca, in1=cv, op=ALU.mult), reads=[caB, cvB], writes=[ab])
                for fc in range(KC):
                    bi = next_bank()
                    specs = [(l, 'w_down', fc, ks, wrot) for ks in range(j0 // KCd, j1 // KCd)]
                    mm_group(bi, specs, lambda kc: aT[:, kc * TB:(kc + 1) * TB], [ab])
                    xt, xb = xrot.next()
                    Sx.add('sp', 'dma_start', dict(out=xt, in_=xres[fc * 128:(fc + 1) * 128, t0:t0 + TB]), writes=[xb], dma=True)
                    so, soB = so_r.next()
                    Sx.add('dve', 'tensor_tensor', dict(out=so, in0=xt, in1=banks[bi][:], op=ALU.add), reads=[xb, bbuf[bi]], writes=[soB])
                    Sx.add('pool', 'dma_start', dict(out=xdst[fc * 128:(fc + 1) * 128, t0:t0 + TB], in_=so), reads=[soB], dma=True)
            Sx.barrier()

        def phase_E(l, xsrc, xdst):
            A = Arena(PBASE)
            hT, hb = A.bf16(KC * TB), Buf()
            NPC = cfg.PLE // 128
            pT, pb_ = A.bf16(NPC * TB), Buf()
            pin_r = Rot([A.f32(cfg.PLE) for _ in range(2)])
            xrot = Rot([A.f32(TB) for _ in range(4)])
            sqrot = Rot([A.bf16(TB) for _ in range(2)])
            AA = {'rstd': (A.f32(TB), Buf())}
            wrot = Rot([A.bf16(32 * 128) for _ in range(4)])
            wprot = Rot([A.bf16(NPC * 128) for _ in range(2)])
            s_r = Rot([A.f32(TB) for _ in range(2)])
            so_r = Rot([A.f32(TB) for _ in range(2)])
            for tb in range(NTB):
                t0 = tb * TB
                norm_pass(AA, xsrc, t0, l, 'g_ple', hT, hb, xrot, sqrot)
                for tt in range(TB // 128):
                    pin, pinB = pin_r.next()
                    Sx.add('sp', 'dma_start', dict(out=pin, in_=p_in[l, t0 + tt * 128:t0 + (tt + 1) * 128, :]), writes=[pinB], dma=True)
                    bi = next_bank()
                    for k in range(NPC):
                        Sx.add('pe', 'transpose', dict(out=banks[bi][:, k * 128:(k + 1) * 128], in_=pin[:, k * 128:(k + 1) * 128], identity=ident_f), reads=[pinB, cb], writes=[bbuf[bi]])
                    for k in range(NPC):
                        Sx.add('act', 'activation', dict(out=pT[:, k * TB + tt * 128:k * TB + (tt + 1) * 128], in_=banks[bi][:, k * 128:(k + 1) * 128], func=AF.Copy), reads=[bbuf[bi]], writes=[pb_])
                for fc in range(KC):
                    gb = next_bank()
                    mm_group(gb, [(l, 'w_pg', fc, 0, wrot)], lambda kc: hT[:, kc * TB:(kc + 1) * TB], [hb])
                    eb = next_bank()
                    mm_group(eb, [(l, 'w_ple', fc, 0, wprot)], lambda kc: pT[:, kc * TB:(kc + 1) * TB], [pb_])
                    s, sB = s_r.next()
                    Sx.add('act', 'activation', dict(out=s, in_=banks[gb][:], func=AF.Sigmoid, bias=vcol(l, 'b_pg', fc), scale=1.0), reads=[bbuf[gb], cb], writes=[sB])
                    Sx.add('dve', 'tensor_tensor', dict(out=s, in0=s, in1=banks[eb][:], op=ALU.mult), reads=[sB, bbuf[eb]], writes=[sB])
                    xt, xb = xrot.next()
                    Sx.add('sp', 'dma_start', dict(out=xt, in_=xsrc[fc * 128:(fc + 1) * 128, t0:t0 + TB]), writes=[xb], dma=True)
                    so, soB = so_r.next()
                    Sx.add('dve', 'tensor_tensor', dict(out=so, in0=xt, in1=s, op=ALU.add), reads=[xb, sB], writes=[soB])
                    Sx.add('pool', 'dma_start', dict(out=xdst[fc * 128:(fc + 1) * 128, t0:t0 + TB], in_=so), reads=[soB], dma=True)
            Sx.barrier()

        def phase_F(xsrc, do_norm):
            A = Arena(PBASE)
            xrot = Rot([A.f32(TB) for _ in range(4)])
            sqrot = Rot([A.bf16(TB) for _ in range(2)])
            rstd, rb = A.f32(TB), Buf()
            yt_r = Rot([A.f32(TB) for _ in range(2)])
            OUT = [A.f32(D) for _ in range(TB // 128)]
            OB = [Buf() for _ in range(TB // 128)]
            for tb in range(NTB):
                t0 = tb * TB
                if do_norm:
                    bi = next_bank()
                    ss, ssb = banks[bi], bbuf[bi]
                    for c in range(KC):
                        xt, xb = xrot.next()
                        Sx.add('sp', 'dma_start', dict(out=xt, in_=xsrc[c * 128:(c + 1) * 128, t0:t0 + TB]), writes=[xb], dma=True)
                        sq, sqb = sqrot.next()
                        Sx.add('act', 'activation', dict(out=sq, in_=xt, func=AF.Square), reads=[xb], writes=[sqb])
                        Sx.add('pe', 'matmul', dict(out=ss[:], lhsT=ones_b, rhs=sq, start=(c == 0), stop=(c == KC - 1)), reads=[sqb, cb], writes=[ssb])
                    Sx.add('act', 'activation', dict(out=rstd, in_=ss[:], func=AF.Sqrt, bias=epsc, scale=1.0 / D), reads=[ssb, cb], writes=[rb])
                    Sx.add('dve', 'reciprocal', dict(out=rstd, in_=rstd), reads=[rb], writes=[rb])
                for c in range(KC):
                    xt, xb = xrot.next()
                    Sx.add('sp', 'dma_start', dict(out=xt, in_=xsrc[c * 128:(c + 1) * 128, t0:t0 + TB]), writes=[xb], dma=True)
                    if do_norm:
                        yt, ytB = yt_r.next()
                        Sx.add('dve', 'scalar_tensor_tensor', dict(out=yt, in0=xt, scalar=vcol(0, 'g_fin', c), in1=rstd, op0=ALU.mult, op1=ALU.mult), reads=[xb, rb, cb], writes=[ytB])
                    else:
                        yt, ytB = xt, xb
                    b2 = next_bank()
                    for tt in range(TB // 128):
                        Sx.add('pe', 'transpose', dict(out=banks[b2][:, tt * 128:(tt + 1) * 128], in_=yt[:, tt * 128:(tt + 1) * 128], identity=ident_f), reads=[ytB, cb], writes=[bbuf[b2]])
                    for tt in range(TB // 128):
                        Sx.add('act', 'activation', dict(out=OUT[tt][:, c * 128:(c + 1) * 128], in_=banks[b2][:, tt * 128:(tt + 1) * 128], func=AF.Copy), reads=[bbuf[b2]], writes=[OB[tt]])
                for tt in range(TB // 128):
                    Sx.add('pool', 'dma_start', dict(out=y_out[t0 + tt * 128:t0 + (tt + 1) * 128, :], in_=OUT[tt]), reads=[OB[tt]], dma=True)
            Sx.barrier()

        phase_input()
        cur = 0
        for l in range(nlayers):
            phase_A(l, XT[cur])
            phase_RG(l)
            phase_ATT(l)
            phase_C(l, XT[cur], XT[1 - cur])
            cur = 1 - cur
            NJ_ = DFF // 128
            _, KSd_, KCd_ = cfg.slab_geom('w_down')
            if KSd_ >= 2:
                jm = (KSd_ // 2) * KCd_
                phase_D(l, XT[cur], XT[cur], XT[2], 0, jm)
                phase_D(l, XT[cur], XT[2], XT[1 - cur], jm, NJ_)
            else:
                phase_D(l, XT[cur], XT[cur], XT[1 - cur], 0, NJ_)
            cur = 1 - cur
            phase_E(l, XT[cur], XT[1 - cur])
            cur = 1 - cur
        phase_F(XT[cur], do_norm)
        Sx.finish()
        Sx.emit(st)
    return nc


def prep_weights(cfg, inp, nlayers):
    out = {}
    vec = np.zeros((128, cfg.depth * cfg.NV), np.float32)

    def put(o, nm, arr):
        a = np.asarray(arr, np.float32)
        vec[:, o + cfg.voff[nm]:o + cfg.voff[nm] + a.shape[1]] = a
    put(0, 'g_fin', colvec(inp['g_final']))
    for l in range(nlayers):
        mats = {
            'w_in': inp['w_in'][l],
            'w_br': inp['w_branch'][l].reshape(cfg.RGW + cfg.SBW, cfg.D),
            'w_bg': inp['w_branch_gate'][l], 'w_out': inp['w_out'][l], 'w_up': inp['w_up'][l],
            'w_down': inp['w_down'][l], 'w_pg': inp['w_ple_gate'][l], 'w_ple': inp['w_ple'][l],
            'w_rga': inp['w_rg_a'][l].transpose(1, 0, 2).reshape(128, cfg.RGW),
            'w_rgx': inp['w_rg_x'][l].transpose(1, 0, 2).reshape(128, cfg.RGW),
        }
        for nm, W in mats.items():
            NF, KS, KCs = cfg.slab_geom(nm)
            out["%s_%d" % (nm, l)] = slabify(np.asarray(W, np.float32), KS, KCs)
        o = l * cfg.NV
        put(o, 'g_mix', colvec(inp['g_mix'][l]))
        put(o, 'g_ffn', colvec(inp['g_ffn'][l]))
        put(o, 'g_ple', colvec(inp['g_ple'][l]))
        put(o, 'b_bg', colvec(inp['b_branch_gate'][l]))
        put(o, 'b_pg', colvec(inp['b_ple_gate'][l]))
        wc = np.asarray(inp['w_rg_conv'][l])
        put(o, 'w_rgc', wc.reshape(4, cfg.NRB, 128).transpose(2, 1, 0).reshape(128, cfg.NRB * 4))
        put(o, 'b_rgc', colvec(inp['b_rg_conv'][l]))
        put(o, 'b_rga', colvec(inp['b_rg_a'][l]))
        put(o, 'b_rgx', colvec(inp['b_rg_x'][l]))
        put(o, 'lam', colvec(inp['rg_lambda'][l]))
        wfc = np.asarray(inp['w_ffn_conv'][l])
        put(o, 'w_fc', wfc.reshape(3, cfg.NFF, 128).transpose(2, 1, 0).reshape(128, cfg.NFF * 3))
        put(o, 'b_fc', colvec(inp['b_ffn_conv'][l]))
    out['vec'] = vec
    return out


def run(cfg, inp, n_cores, debug_taps=(), nlayers=None, do_norm=True):
    if nlayers is None:
        nlayers = cfg.depth
    B = inp['x'].shape[0]
    nc = build_program(cfg, debug_taps, nlayers, do_norm)
    shared = prep_weights(cfg, inp, nlayers)
    in_maps = []
    for c in range(n_cores):
        b = c % B
        m = dict(shared)
        m['x'] = np.ascontiguousarray(np.asarray(inp['x'][b], np.float32))
        m['p'] = np.ascontiguousarray(np.asarray(inp['p'][:cfg.depth, b], np.float32))
        in_maps.append(m)
    res = run_bass_kernel_spmd(nc, in_maps, core_ids=list(range(n_cores)))
    return res


LAYER_KEYS = ['p', 'g_mix', 'w_in', 'w_rg_conv', 'b_rg_conv', 'w_rg_a', 'b_rg_a', 'w_rg_x', 'b_rg_x', 'rg_lambda',
              'w_branch', 'w_branch_gate', 'b_branch_gate', 'w_out', 'g_ffn', 'w_up', 'w_ffn_conv', 'b_ffn_conv',
              'w_down', 'g_ple', 'w_ple', 'w_ple_gate', 'b_ple_gate']


def kernel_unfused(inp, cfg1, depth, n_cores):
    B = inp['x'].shape[0]
    x = inp['x']
    for l in range(depth):
        sub = {k: inp[k][l:l + 1] for k in LAYER_KEYS}
        sub['x'] = x
        sub['g_final'] = inp['g_final']
        res = run(cfg1, sub, n_cores, nlayers=1, do_norm=False)
        x = np.stack([np.asarray(res.results[b]["y"], np.float32) for b in range(B)], axis=0)
    sub = {'x': x, 'p': inp['p'][0:1], 'g_final': inp['g_final']}
    res = run(cfg1, sub, n_cores, nlayers=0, do_norm=True)
    return np.stack([np.asarray(res.results[b]["y"], np.float32) for b in range(B)], axis=0)


def kernel(**inputs):
    inp = {k: np.asarray(v) for k, v in inputs.items()}
    depth = inp['w_in'].shape[0]
    B = inp['x'].shape[0]
    res = run(Cfg(depth=depth), inp, 4)
    return np.stack([np.asarray(res.results[b]["y"], np.float32) for b in range(B)], axis=0)
```
